# Optimizing a Trainium2 kernel written in Bass

```python
import jax
import jax.numpy as jnp
from jax import lax
import numpy as np

D_MODEL = 2048
BATCH = 4
SEQ = 2048
DEPTH = 4
DEC_BATCH = 8
DEC_SEQ = 4
PAST_LEN = 16384
PAGE_SIZE = 128

N_MIXERS = 3
N_POOL_LAYERS = (DEPTH + 2) // 3
N_CHUNK_LAYERS = (DEPTH + 1) // 3
N_ATTN_LAYERS = DEPTH // 3

D_FF = 5504
RMS_EPS = 1e-6

POOL_WINDOWS = (2, 4, 8, 16)
POOL_GROUP = D_MODEL // len(POOL_WINDOWS)
POOL_BUF = max(POOL_WINDOWS) - 1

CHUNK = 128
CHUNK_WIDTH = D_MODEL
CHUNK_GROUPS = 8
CHUNK_GROUP_W = CHUNK_WIDTH // CHUNK_GROUPS

ATTN_GROUPS = ((128, 1), (512, 4), (2048, 16))
N_ATTN_GROUPS = len(ATTN_GROUPS)
HEADS_PER_GROUP = 16
HEAD_DIM = 128
ROPE_THETA = 10000.0
ATTN_SCALE = HEAD_DIM ** -0.5
NEG = float(np.finfo(np.float32).min)

kernel_name = 'hybrid_pool_chunkgmlp_dilated_attn_decode_step'


def rms_norm(x, g):
    x32 = x.astype(jnp.float32)
    y = x32 * lax.rsqrt(jnp.mean(x32 * x32, axis=-1, keepdims=True) + RMS_EPS)
    return (y * g.astype(jnp.float32)).astype(x.dtype)


def swiglu(h, w_in, w_out):
    gate, up = jnp.split(h @ w_in, 2, axis=-1)
    return (jax.nn.silu(gate) * up) @ w_out


def pool_mixer(h, buf, start, w_pool, scale):
    B, T, _ = h.shape
    hb = jnp.concatenate([buf.astype(h.dtype), h], axis=1)
    hb32 = hb.astype(jnp.float32)
    cs = jnp.concatenate([jnp.zeros_like(hb32[:, :1]), jnp.cumsum(hb32, axis=1)], axis=1)
    pos = start + jnp.arange(T)
    h32 = h.astype(jnp.float32)
    hi = POOL_BUF + 1
    groups = []
    for g, w in enumerate(POOL_WINDOWS):
        sl = slice(g * POOL_GROUP, (g + 1) * POOL_GROUP)
        win_sum = cs[:, hi:hi + T, sl] - cs[:, hi - w:hi - w + T, sl]
        count = jnp.minimum(w, pos + 1).astype(jnp.float32)
        groups.append(win_sum / count[None, :, None] - h32[:, :, sl])
    pooled = jnp.stack(groups, axis=2)
    mixed = jnp.einsum('btgc,gcd->btgd', pooled, w_pool.astype(jnp.float32))
    out = mixed.reshape(B, T, D_MODEL) * scale.astype(jnp.float32)
    return out.astype(h.dtype), hb[:, -POOL_BUF:]


def chunk_proj(h, w_in, v_gain):
    u, v = jnp.split(jax.nn.gelu(h @ w_in, approximate=False), 2, axis=-1)
    return u, rms_norm(v, v_gain)


def causal_ws(w_s):
    mask = jnp.tril(jnp.ones((CHUNK, CHUNK), dtype=bool))
    return jnp.where(mask[None], w_s, jnp.zeros_like(w_s))


def chunk_mixer_prompt(h, w_in, v_gain, w_s, b_s, w_out):
    B, S, _ = h.shape
    u, v = chunk_proj(h, w_in, v_gain)
    vc = v.reshape(B, S // CHUNK, CHUNK, CHUNK_GROUPS, CHUNK_GROUP_W)
    mixed = jnp.einsum('bncgd,gqc->bnqgd', vc, causal_ws(w_s).astype(v.dtype))
    mixed = mixed + b_s.T.astype(v.dtype)[None, None, :, :, None]
    return (u * mixed.reshape(B, S, CHUNK_WIDTH)) @ w_out


def chunk_mixer_sample(h, w_in, v_gain, w_s, b_s, w_out):
    B, T, _ = h.shape
    u, v = chunk_proj(h, w_in, v_gain)
    ws = causal_ws(w_s)[:, :T, :T].astype(v.dtype)
    mixed = jnp.einsum('btgd,gqt->bqgd', v.reshape(B, T, CHUNK_GROUPS, CHUNK_GROUP_W), ws)
    mixed = mixed + b_s[:, :T].T.astype(v.dtype)[None, :, :, None]
    return (u * mixed.reshape(B, T, CHUNK_WIDTH)) @ w_out, v


def rope(x, pos):
    half = HEAD_DIM // 2
    freqs = ROPE_THETA ** (-2.0 * jnp.arange(half, dtype=jnp.float32) / HEAD_DIM)
    ang = pos.astype(jnp.float32)[:, None] * freqs[None, :]
    cos = jnp.cos(ang)[None, :, None, None, :]
    sin = jnp.sin(ang)[None, :, None, None, :]
    x32 = x.astype(jnp.float32)
    x1, x2 = x32[..., :half], x32[..., half:]
    return jnp.concatenate([x1 * cos - x2 * sin, x1 * sin + x2 * cos], axis=-1).astype(x.dtype)


def attn_qkv(h, pos, w_qkv, q_gain, k_gain):
    B, T, _ = h.shape
    qkv = (h @ w_qkv).reshape(B, T, 3, N_ATTN_GROUPS, HEADS_PER_GROUP, HEAD_DIM)
    q = rope(rms_norm(qkv[:, :, 0], q_gain), pos)
    k = rope(rms_norm(qkv[:, :, 1], k_gain), pos)
    return q, k, qkv[:, :, 2]


def dilated_attn_prompt(q, k, v, window, dil):
    B, S, H, E = q.shape
    R = window // dil
    n = S // dil
    nb = -(-n // R)
    n_pad = nb * R

    def to_blocks(x):
        x = x.reshape(B, n, dil, H, E).transpose(0, 2, 1, 3, 4)
        x = jnp.pad(x, ((0, 0), (0, 0), (0, n_pad - n), (0, 0), (0, 0)))
        return x.reshape(B, dil, nb, R, H, E)

    def with_prev(x):
        prev = jnp.pad(x, ((0, 0), (0, 0), (1, 0), (0, 0), (0, 0), (0, 0)))[:, :, :nb]
        return jnp.concatenate([prev, x], axis=3)

    qb = to_blocks(q)
    kk = with_prev(to_blocks(k))
    vv = with_prev(to_blocks(v))
    s = jnp.einsum('brnqhe,brnkhe->brnhqk', qb, kk, preferred_element_type=jnp.float32) * ATTN_SCALE
    qi = jnp.arange(R)[:, None]
    kj = jnp.arange(2 * R)[None, :]
    dist = R + qi - kj
    band = (dist >= 0) & (dist <= R)
    key_m = (jnp.arange(nb) * R - R)[:, None, None] + kj[None]
    mask = band[None] & (key_m >= 0)
    s = jnp.where(mask[None, None, :, None], s, NEG)
    lse = jax.nn.logsumexp(s, axis=-1)
    p = jnp.exp(s - lse[..., None])
    o = jnp.einsum('brnhqk,brnkhe->brnqhe', p, vv.astype(jnp.float32))
    o = o.reshape(B, dil, n_pad, H, E)[:, :, :n].transpose(0, 2, 1, 3, 4).reshape(B, S, H, E)
    lse = lse.transpose(0, 1, 2, 4, 3).reshape(B, dil, n_pad, H)[:, :, :n]
    lse = lse.transpose(0, 2, 1, 3).reshape(B, S, H)
    return o, lse


def dilated_attn_sample(q, k_new, v_new, cache_kv, window, dil):
    B, T, H, E = q.shape
    L = cache_kv.shape[1]
    R = window // dil
    k_all = jnp.concatenate([cache_kv[:, :, 0].astype(k_new.dtype), k_new], axis=1)
    v_all = jnp.concatenate([cache_kv[:, :, 1].astype(v_new.dtype), v_new], axis=1)
    idx = L + jnp.arange(T)[:, None] - dil * jnp.arange(R + 1)[None, :]
    valid = idx >= 0
    idx_c = jnp.clip(idx, 0, L + T - 1)
    kg = k_all[:, idx_c]
    vg = v_all[:, idx_c]
    s = jnp.einsum('bthe,btkhe->bthk', q, kg, preferred_element_type=jnp.float32) * ATTN_SCALE
    s = jnp.where(valid[None, :, None, :], s, NEG)
    lse = jax.nn.logsumexp(s, axis=-1)
    p = jnp.exp(s - lse[..., None])
    o = jnp.einsum('bthk,btkhe->bthe', p, vg.astype(jnp.float32))
    return o, lse


def combine_groups(outs, lses, w_out, dtype):
    alpha = jax.nn.softmax(jnp.stack(lses, axis=0), axis=0)
    o = jnp.einsum('gbth,gbthe->bthe', alpha, jnp.stack(outs, axis=0).astype(jnp.float32))
    B, T = o.shape[:2]
    return o.reshape(B, T, HEADS_PER_GROUP * HEAD_DIM).astype(dtype) @ w_out


def attn_mixer_prompt(h, w_qkv, q_gain, k_gain, w_out):
    B, S, _ = h.shape
    q, k, v = attn_qkv(h, jnp.arange(S), w_qkv, q_gain, k_gain)
    outs, lses, rows = [], [], []
    for g, (window, dil) in enumerate(ATTN_GROUPS):
        o, l = dilated_attn_prompt(q[:, :, g], k[:, :, g], v[:, :, g], window, dil)
        outs.append(o)
        lses.append(l)
        keep = min(window, S)
        rows.append(jnp.stack([k[:, S - keep:, g], v[:, S - keep:, g]], axis=2))
    return combine_groups(outs, lses, w_out, h.dtype), rows


def attn_mixer_sample(h, caches, w_qkv, q_gain, k_gain, w_out):
    B, T, _ = h.shape
    q, k, v = attn_qkv(h, PAST_LEN + jnp.arange(T), w_qkv, q_gain, k_gain)
    outs, lses, rows = [], [], []
    for g, (window, dil) in enumerate(ATTN_GROUPS):
        o, l = dilated_attn_sample(q[:, :, g], k[:, :, g], v[:, :, g], caches[g], window, dil)
        outs.append(o)
        lses.append(l)
        rows.append(jnp.stack([k[:, :, g], v[:, :, g]], axis=2))
    return combine_groups(outs, lses, w_out, h.dtype), rows


def setup_inputs(seed: int = 0) -> dict:
    key = jax.random.key(seed)
    ks = jax.random.split(key, 32)
    f32 = jnp.float32

    def nrm(k, shape, scale):
        return jax.random.normal(k, shape, f32) * scale

    def gain(k, shape, noise):
        return 1.0 + noise * jax.random.normal(k, shape, f32)

    attn_w = N_ATTN_GROUPS * HEADS_PER_GROUP * HEAD_DIM
    inp = {}
    inp['x_prompt'] = nrm(ks[0], (BATCH, SEQ, D_MODEL), 1.0)
    inp['x_sample'] = nrm(ks[1], (DEC_BATCH, DEC_SEQ, D_MODEL), 1.0)
    inp['state_pool'] = nrm(ks[2], (N_POOL_LAYERS, DEC_BATCH, POOL_BUF, D_MODEL), 1.0)
    for g, (window, _) in enumerate(ATTN_GROUPS):
        keep = min(window, PAST_LEN)
        inp['cache_kv_g%d' % g] = nrm(ks[3 + g], (N_ATTN_LAYERS, DEC_BATCH, keep, 2, HEADS_PER_GROUP, HEAD_DIM), 1.0)
    inp['norm_ffn1'] = gain(ks[6], (DEPTH, D_MODEL), 0.05)
    inp['ffn1_w_in'] = nrm(ks[7], (DEPTH, D_MODEL, 2 * D_FF), D_MODEL ** -0.5)
    inp['ffn1_w_out'] = nrm(ks[8], (DEPTH, D_FF, D_MODEL), D_FF ** -0.5)
    inp['norm_mix'] = gain(ks[9], (DEPTH, D_MODEL), 0.05)
    inp['norm_ffn2'] = gain(ks[10], (DEPTH, D_MODEL), 0.05)
    inp['ffn2_w_in'] = nrm(ks[11], (DEPTH, D_MODEL, 2 * D_FF), D_MODEL ** -0.5)
    inp['ffn2_w_out'] = nrm(ks[12], (DEPTH, D_FF, D_MODEL), D_FF ** -0.5)
    inp['pool_w'] = nrm(ks[13], (N_POOL_LAYERS, len(POOL_WINDOWS), POOL_GROUP, POOL_GROUP), POOL_GROUP ** -0.5)
    inp['pool_scale'] = gain(ks[14], (N_POOL_LAYERS, D_MODEL), 0.1)
    inp['chunk_w_in'] = nrm(ks[15], (N_CHUNK_LAYERS, D_MODEL, 2 * CHUNK_WIDTH), D_MODEL ** -0.5)
    inp['chunk_v_norm'] = gain(ks[16], (N_CHUNK_LAYERS, CHUNK_WIDTH), 0.05)
    inp['chunk_w_s'] = nrm(ks[17], (N_CHUNK_LAYERS, CHUNK_GROUPS, CHUNK, CHUNK), CHUNK ** -0.5)
    inp['chunk_b_s'] = gain(ks[18], (N_CHUNK_LAYERS, CHUNK_GROUPS, CHUNK), 0.1)
    inp['chunk_w_out'] = nrm(ks[19], (N_CHUNK_LAYERS, CHUNK_WIDTH, D_MODEL), CHUNK_WIDTH ** -0.5)
    inp['attn_w_qkv'] = nrm(ks[20], (N_ATTN_LAYERS, D_MODEL, 3 * attn_w), D_MODEL ** -0.5)
    inp['attn_q_norm'] = gain(ks[21], (N_ATTN_LAYERS, HEAD_DIM), 0.05)
    inp['attn_k_norm'] = gain(ks[22], (N_ATTN_LAYERS, HEAD_DIM), 0.05)
    inp['attn_w_out'] = nrm(ks[23], (N_ATTN_LAYERS, HEADS_PER_GROUP * HEAD_DIM, D_MODEL), (HEADS_PER_GROUP * HEAD_DIM) ** -0.5)
    return inp


def reference(x_prompt, x_sample, state_pool, cache_kv_g0, cache_kv_g1, cache_kv_g2,
              norm_ffn1, ffn1_w_in, ffn1_w_out, norm_mix, norm_ffn2, ffn2_w_in, ffn2_w_out,
              pool_w, pool_scale,
              chunk_w_in, chunk_v_norm, chunk_w_s, chunk_b_s, chunk_w_out,
              attn_w_qkv, attn_q_norm, attn_k_norm, attn_w_out):
    caches = (cache_kv_g0, cache_kv_g1, cache_kv_g2)
    xp, xs = x_prompt, x_sample
    pool_p, pool_s, chunk_s = [], [], []
    kv_p = [[] for _ in ATTN_GROUPS]
    kv_s = [[] for _ in ATTN_GROUPS]
    for i in range(DEPTH):
        kind, j = i % N_MIXERS, i // N_MIXERS
        xp = xp + 0.5 * swiglu(rms_norm(xp, norm_ffn1[i]), ffn1_w_in[i], ffn1_w_out[i])
        xs = xs + 0.5 * swiglu(rms_norm(xs, norm_ffn1[i]), ffn1_w_in[i], ffn1_w_out[i])
        hp = rms_norm(xp, norm_mix[i])
        hs = rms_norm(xs, norm_mix[i])
        if kind == 0:
            zero_buf = jnp.zeros((hp.shape[0], POOL_BUF, D_MODEL), hp.dtype)
            mp, st_p = pool_mixer(hp, zero_buf, 0, pool_w[j], pool_scale[j])
            ms, st_s = pool_mixer(hs, state_pool[j], PAST_LEN, pool_w[j], pool_scale[j])
            pool_p.append(st_p)
            pool_s.append(st_s)
        elif kind == 1:
            mp = chunk_mixer_prompt(hp, chunk_w_in[j], chunk_v_norm[j], chunk_w_s[j], chunk_b_s[j], chunk_w_out[j])
            ms, v_new = chunk_mixer_sample(hs, chunk_w_in[j], chunk_v_norm[j], chunk_w_s[j], chunk_b_s[j], chunk_w_out[j])
            chunk_s.append(v_new)
        else:
            mp, rows_p = attn_mixer_prompt(hp, attn_w_qkv[j], attn_q_norm[j], attn_k_norm[j], attn_w_out[j])
            ms, rows_s = attn_mixer_sample(hs, tuple(c[j] for c in caches), attn_w_qkv[j], attn_q_norm[j], attn_k_norm[j], attn_w_out[j])
            for g in range(N_ATTN_GROUPS):
                kv_p[g].append(rows_p[g])
                kv_s[g].append(rows_s[g])
        xp = xp + mp
        xs = xs + ms
        xp = xp + 0.5 * swiglu(rms_norm(xp, norm_ffn2[i]), ffn2_w_in[i], ffn2_w_out[i])
        xs = xs + 0.5 * swiglu(rms_norm(xs, norm_ffn2[i]), ffn2_w_in[i], ffn2_w_out[i])
    return (xp, xs, jnp.stack(pool_p), jnp.stack(pool_s), jnp.stack(chunk_s),
            jnp.stack(kv_p[0]), jnp.stack(kv_s[0]), jnp.stack(kv_p[1]), jnp.stack(kv_s[1]),
            jnp.stack(kv_p[2]), jnp.stack(kv_s[2]))
```

```python
import numpy as np
import concourse.bass as bass
import concourse.mybir as mybir
from concourse.bass_utils import run_bass_kernel_spmd

F32 = mybir.dt.float32
BF16 = mybir.dt.bfloat16
AF = mybir.ActivationFunctionType
ALU = mybir.AluOpType

EPS = 1e-6
M = 16
TB = 1024
NS = 4
SEQ = 2048
PAST = 16384
NSLOT = 4
SLOTW = 4096
ND = 24
WINS = (128, 512, 2048)
DILS = (1, 4, 16)
NDEL = (2, 5, 16)
KEEP = (128, 512, 2048)
MASK_BASE = {0: (0, None), 1: (2, 6), 2: (7, 8)}
DBG = set()
PCORE = (0, 1, 4, 5)


class Cfg:
    def __init__(self, D=2048, DFF=5504, H=16, DEPTH=4, NCORES=8):
        self.D, self.DFF, self.H, self.DEPTH, self.NCORES = D, DFF, H, DEPTH, NCORES
        self.KC = D // 128
        self.NJ = DFF // 128
        self.AW = 3 * H * 128
        self.NPL = (DEPTH + 2) // 3
        self.NCL = (DEPTH + 1) // 3
        self.NAL = DEPTH // 3
        self.NV = 3 * DEPTH + self.NPL + self.NCL


ENG = ("pe", "act", "dve", "pool", "sp")


class Em:
    def __init__(self):
        self.ops = {e: [] for e in ENG}
        self.cnt = {e: 0 for e in ENG}
        self.seen = {e: {} for e in ENG}
        self.last = {e: None for e in ENG}
        self.ndma = 0
        self.dval = [0] * ND

    def op(self, eng, fn, deps=(), sig=True):
        waits = []
        for d in deps:
            if d is None:
                continue
            key, val = d
            if self.seen[eng].get(key, 0) >= val:
                continue
            self.seen[eng][key] = val
            waits.append((key, val))
        tok = None
        if sig:
            self.cnt[eng] += 1
            tok = (("e", eng), self.cnt[eng])
            self.last[eng] = tok
        self.ops[eng].append((fn, waits, sig))
        return tok

    def dma(self, out, in_, deps=(), q="sp"):
        i = self.ndma % ND
        self.ndma += 1
        key = ("d", i)
        prev = self.dval[i]
        self.dval[i] += 16
        deps = list(deps)
        if prev:
            deps.append((key, prev))
        val = self.dval[i]
        self.op(q, lambda e, o=out, s=in_, k=key: ("dma", o, s, k), deps=deps, sig=False)
        return (key, val)

    def all_dma_toks(self):
        return [(("d", i), v) for i, v in enumerate(self.dval) if v]

    def barrier(self):
        toks = [self.last[e] for e in ("pe", "act", "dve")] + self.all_dma_toks()
        for e in ("pe", "act", "dve", "sp"):
            for d in toks:
                if d is None:
                    continue
                key, val = d
                if self.seen[e].get(key, 0) >= val:
                    continue
                self.seen[e][key] = val
                self.ops[e].append((None, [(key, val)], False))


class Rot:
    def __init__(self, aps):
        self.aps = list(aps)
        self.last = [[] for _ in self.aps]
        self.i = 0

    def next(self):
        i = self.i % len(self.aps)
        self.i += 1
        return i, self.aps[i], list(self.last[i])

    def done(self, i, toks):
        self.last[i] = [t for t in toks if t is not None]


class Ring:
    def __init__(self, em, slots):
        self.em, self.slots = em, slots
        self.loads, self.rel = [], []

    def get(self, src, view, deps=()):
        i = len(self.loads)
        slot = i % NSLOT
        dst = view(self.slots[slot])
        self.loads.append((src, dst, slot, list(deps)))
        self.rel.append(None)
        return dst, (("w", slot), 16 * (i // NSLOT + 1)), i

    def release(self, i, tok):
        assert tok is not None
        self.rel[i] = tok

    def finalize(self):
        for i, (src, dst, slot, xd) in enumerate(self.loads):
            assert self.rel[i] is not None, i
            deps = xd + ([self.rel[i - NSLOT]] if i >= NSLOT else [])
            self.em.op("pool", lambda e, o=dst, s=src, k=("w", slot): ("dma", o, s, k), deps=deps, sig=False)


def build(cfg, upto=None):
    D, KC, NJ, H = cfg.D, cfg.KC, cfg.NJ, cfg.H
    NPL, NCL, NAL = cfg.NPL, cfg.NCL, cfg.NAL
    nc = bass.Bass("TRN2", target_bir_lowering=False)

    def din(name, shape):
        return nc.dram_tensor(name, list(shape), F32, kind="ExternalInput").ap()

    def dout(name, shape):
        return nc.dram_tensor(name, list(shape), F32, kind="ExternalOutput").ap()

    x_in = din("x_in", [D, SEQ])
    xs_in = din("xs_in", [D, NS])
    vecs_in = din("vecs", [128, cfg.NV, KC])
    consts_in = din("consts", [128, 320])
    cmask_in = din("cmask", [128, 1740])
    rope_in = din("rope", [128, 2, SEQ + NS])
    w1i = din("w1i", [cfg.DEPTH, NJ, 128, 2 * KC * 128])
    w1o = din("w1o", [cfg.DEPTH, NJ, 128, D])
    w2i = din("w2i", [cfg.DEPTH, NJ, 128, 2 * KC * 128])
    w2o = din("w2o", [cfg.DEPTH, NJ, 128, D])
    pw_in = din("pool_w", [NPL, 4, D // 4, D // 4])
    pstate_in = din("pool_state", [NPL, D, 15])
    cwin_v = din("cw_v", [NCL, D, D])
    cwin_u = din("cw_u", [NCL, KC, 128, KC * 128])
    cwo = din("cw_o", [NCL, KC, 128, KC * 128])
    cvg_in = din("c_vg", [NCL, D])
    cws_in = din("c_ws", [NCL, 128, 8 * 128])
    cbs_in = din("c_bs", [NCL, 8 * 128])
    aqkv = din("a_qkv", [NAL, 9 * H, 128, KC * 128])
    awo = din("a_wo", [NAL, KC, 128, H * 128])
    agn_in = din("a_gn", [NAL, 128, 2])
    kc_in = [din("kcache%d" % g, [NAL, H, 128, WINS[g]]) for g in range(3)]
    vc_in = [din("vcache%d" % g, [NAL, H, 128, WINS[g]]) for g in range(3)]

    y_out = dout("y", [D, SEQ])
    ys_out = dout("ys", [D, NS])
    pp_out = dout("pool_p", [NPL, D, 15])
    ps_out = dout("pool_s", [NPL, D, 15])
    cv_out = dout("chunk_v", [NCL, NS, D])
    ko = [dout("ko%d" % g, [NAL, H, 128, KEEP[g]]) for g in range(3)]
    vo = [dout("vo%d" % g, [NAL, KEEP[g], H, 128]) for g in range(3)]
    kos = [dout("kos%d" % g, [NAL, H, 128, NS]) for g in range(3)]
    vos = [dout("vos%d" % g, [NAL, NS, H, 128]) for g in range(3)]
    kscr = dout("kscr", [3, H, 128, TB])
    vscr = dout("vscr", [3, H, 128, 8 * 128])

    NT0 = TB + NS
    em = Em()
    S32 = 6160
    S16 = 7176
    import contextlib
    with contextlib.ExitStack() as st:
        xT = st.enter_context(nc.sbuf_tensor("xT", [128, KC, NT0], F32))
        hT = st.enter_context(nc.sbuf_tensor("hT", [128, KC, M + NT0], BF16))
        AB = st.enter_context(nc.sbuf_tensor("AB", [128, 16 * NT0], BF16))
        RG = st.enter_context(nc.sbuf_tensor("RG", [128, NSLOT, SLOTW], BF16))
        s32 = st.enter_context(nc.sbuf_tensor("s32", [128, S32], F32))
        s16 = st.enter_context(nc.sbuf_tensor("s16", [128, S16], BF16))
        vsb2 = st.enter_context(nc.sbuf_tensor("vsb2", [128, 3 * 128], BF16))
        vsb = vsb2[:, :].rearrange("p (g e) -> p g e", g=3)
        vecs = st.enter_context(nc.sbuf_tensor("vecs_sb", [128, cfg.NV, KC], F32))
        cst = st.enter_context(nc.sbuf_tensor("cst", [128, 320], F32))
        ones = st.enter_context(nc.sbuf_tensor("ones", [128, 128], BF16))
        zer = st.enter_context(nc.sbuf_tensor("zer", [128, 128], BF16))
        masks = st.enter_context(nc.sbuf_tensor("masks", [128, 12, 128], BF16))
        smaskb = st.enter_context(nc.sbuf_tensor("smaskb", [128, 3, 64], BF16))
        nmaskb = st.enter_context(nc.sbuf_tensor("nmaskb", [128, 3, 4], BF16))
        halo = st.enter_context(nc.sbuf_tensor("halo", [128, max(NPL, 1), KC, 15], BF16))
        agn = st.enter_context(nc.sbuf_tensor("agn", [128, max(NAL, 1), 2], F32))
        smallb = st.enter_context(nc.sbuf_tensor("smallb", [128, 3, 3, 4], BF16))
        sm32 = st.enter_context(nc.sbuf_tensor("sm32", [128, 40], F32))
        PS = st.enter_context(nc.psum_tensor("PS", [128, 8, 512], F32))
        esem = {e: st.enter_context(nc.semaphore("sem_" + e)) for e in ("pe", "act", "dve")}
        dsem = [st.enter_context(nc.semaphore("dsem%d" % i)) for i in range(ND)]
        wsem = [st.enter_context(nc.semaphore("wsem%d" % i)) for i in range(NSLOT)]
        block = st.enter_context(nc.Block())

        ring = Ring(em, [RG[:, i, :] for i in range(NSLOT)])
        C_PSW = cst[:, 0:128]
        C_INVC = cst[:, 128:192]
        C_TRI = cst[:, 192:320]
        C_MASK = s32[:, 0:1536]
        C_SM = s32[:, 1536:1536 + 192]
        C_NM = s32[:, 1728:1728 + 12]

        def V(idx, kc):
            return vecs[:, idx, kc:kc + 1]

        def act(fn, deps=()):
            return em.op("act", fn, deps)

        def dve(fn, deps=()):
            return em.op("dve", fn, deps)

        def mm_group(out, pairs, deps=(), sig=True, flags=None):
            n = len(pairs)
            tok = None
            for i, (l, r) in enumerate(pairs):
                s0 = (i == 0) if flags is None else flags[0] and i == 0
                s1 = (i == n - 1) if flags is None else flags[1] and i == n - 1
                tok = em.op("pe", lambda t, o=out, l=l, r=r, a=s0, b=s1, sk=(flags is not None): t.matmul(o, l, r, start=a, stop=b, skip_group_check=sk),
                            deps=deps if i == 0 else (), sig=(sig and i == n - 1))
            return tok

        def bank(i):
            return PS[:, i, :]

        def tiles(pas):
            if pas == 0:
                return [(0, 343), (343, 343), (686, 342)]
            return [(0, 512), (512, 512)]

        def atiles(pas):
            t = [(0, 512), (512, 512)]
            if pas == 0:
                t.append((TB, NS))
            return t

        em.dma(cst[:, :], consts_in[:, :])
        t_c = em.dma(s32[:, 0:1740], cmask_in[:, :])
        t_v = em.dma(vecs[:, :, :], vecs_in[:, :, :])
        if NAL:
            t_g = em.dma(agn[:, :, :], agn_in.rearrange("l p t -> p l t"))
        dve(lambda e: e.memset(ones[:, :], 1.0))
        dve(lambda e: e.memset(zer[:, :], 0.0))
        dve(lambda e: e.tensor_copy(out=masks[:, :, :], in_=C_MASK.rearrange("p (a b) -> p a b", a=12)), [t_c])
        dve(lambda e: e.tensor_copy(out=smaskb[:, :, :], in_=C_SM.rearrange("p (a b) -> p a b", a=3)), [t_c])
        dve(lambda e: e.tensor_copy(out=nmaskb[:, :, :], in_=C_NM.rearrange("p (a b) -> p a b", a=3)), [t_c])
        dve(lambda e: e.memset(hT[:, :, 0:M], 0.0))
        em.barrier()

        rstd = s32[:, 0:NT0]
        FS = 1032

        def rmsnorm(pas, vidx):
            NT = TB + (NS if pas == 0 else 0)
            ta = None
            for kc in range(KC):
                ta = act(lambda e, kc=kc: e.activation(out=hT[:, kc, M:M + NT], in_=xT[:, kc, 0:NT], func=AF.Square))
            rot = Rot([bank(6), bank(7)])
            tds = []
            for (c0, w) in tiles(pas):
                i, ps, lst = rot.next()
                tp = mm_group(ps[:, :w], [(ones[:, :], hT[:, kc, M + c0:M + c0 + w]) for kc in range(KC)], deps=[ta] + lst)
                td = act(lambda e, ps=ps, c0=c0, w=w: e.activation(out=rstd[:, c0:c0 + w], in_=ps[:, :w], func=AF.Sqrt,
                                                                    scale=1.0 / D, bias=EPS), [tp])
                td = dve(lambda e, c0=c0, w=w: e.reciprocal(out=rstd[:, c0:c0 + w], in_=rstd[:, c0:c0 + w]), [td])
                rot.done(i, [td])
                tds.append(td)
                tpl = tp
            for kc in range(KC):
                dve(lambda e, kc=kc: e.scalar_tensor_tensor(out=hT[:, kc, M:M + NT], in0=xT[:, kc, 0:NT], scalar=V(vidx, kc),
                                                            in1=rstd[:, 0:NT], op0=ALU.mult, op1=ALU.mult), [tpl] + tds)
            em.barrier()

        def ffn(pas, wi, wo):
            NT = TB + (NS if pas == 0 else 0)
            GS = 6
            groups = [list(range(a, min(a + GS, NJ))) for a in range(0, NJ, GS)]
            abrot = Rot([AB[:, 0:8 * NT0].rearrange("p (a t) -> p a t", a=8), AB[:, 8 * NT0:16 * NT0].rearrange("p (a t) -> p a t", a=8)])
            mmrot = Rot([(bank(0), bank(1)), (bank(2), bank(3))])
            accrot = Rot([bank(4), bank(5)])
            sgrot = Rot([s32[:, FS:FS + 512], s32[:, FS + 512:FS + 1024]])
            def win_phase(grp):
                ai, ab, alast = abrot.next()
                tdl = None
                for jj, j in enumerate(grp):
                    slot, wtok, li = ring.get(wi[j], lambda s: s[:, 0:2 * KC * 128].rearrange("p (a k m) -> p a k m", a=2, k=KC))
                    tu = None
                    for (c0, w) in tiles(pas):
                        pi, (pg, pu), plast = mmrot.next()
                        mm_group(pg[:, :w], [(slot[:, 0, kc, :], hT[:, kc, M + c0:M + c0 + w]) for kc in range(KC)],
                                 deps=[wtok] + plast, sig=False)
                        tu = mm_group(pu[:, :w], [(slot[:, 1, kc, :], hT[:, kc, M + c0:M + c0 + w]) for kc in range(KC)])
                        si, sg, slast = sgrot.next()
                        ta = act(lambda e, sg=sg, pg=pg, w=w: e.activation(out=sg[:, :w], in_=pg[:, :w], func=AF.Silu), [tu] + slast)
                        tdl = dve(lambda e, ab=ab, jj=jj, c0=c0, w=w, sg=sg, pu=pu: e.tensor_tensor(
                            out=ab[:, jj, c0:c0 + w], in0=sg[:, :w], in1=pu[:, :w], op=ALU.mult), [ta] + alast)
                        mmrot.done(pi, [tdl])
                        sgrot.done(si, [tdl])
                    ring.release(li, tu)
                return (grp, ai, ab, tdl)

            def wout_phase(stt):
                grp, ai, ab, tdl = stt
                wsl = []
                for a in range(0, len(grp), 2):
                    n = min(2, len(grp) - a)
                    j0 = grp[a]
                    slot, wtok, li = ring.get(wo[j0:j0 + n].rearrange("a p n -> p a n"),
                                              lambda s, n=n: s[:, 0:n * D].rearrange("p (a n) -> p a n", a=n))
                    wsl.append((slot, wtok, li))
                tl = None
                for m in range(KC):
                    for (c0, w) in tiles(pas):
                        ci, pa, clast = accrot.next()
                        tl = mm_group(pa[:, :w], [(wsl[jj // 2][0][:, jj % 2, m * 128:(m + 1) * 128], ab[:, jj, c0:c0 + w])
                                                  for jj in range(len(grp))],
                                      deps=[x[1] for x in wsl] + [tdl] + clast)
                        tx = dve(lambda e, pa=pa, m=m, c0=c0, w=w: e.scalar_tensor_tensor(
                            out=xT[:, m, c0:c0 + w], in0=pa[:, :w], scalar=0.5, in1=xT[:, m, c0:c0 + w],
                            op0=ALU.mult, op1=ALU.add), [tl])
                        accrot.done(ci, [tx])
                for (_, _, li) in wsl:
                    ring.release(li, tl)
                abrot.done(ai, [tl])

            pend = None
            for grp in groups:
                stt = win_phase(grp)
                if pend is not None:
                    wout_phase(pend)
                pend = stt
            wout_phase(pend)
            em.barrier()

        def pool_mixer(pas, jp, vmix, vscale):
            NT = TB + (NS if pas == 0 else 0)
            KG = KC // 4
            PG = D // 4
            C = M + TB
            T1 = s32[:, FS:FS + C]
            T2 = s32[:, FS + C:FS + 2 * C]
            HB = s32[:, FS + 2 * C:FS + 2 * C + KC * 19].rearrange("p (k c) -> p k c", k=KC)
            TS = s32[:, FS + 2 * C + KC * 19:FS + 2 * C + KC * 19 + 64]
            P32 = s32[:, FS + 2 * C + KC * 19 + 64:FS + 2 * C + KC * 19 + 64 + KC * 15].rearrange("p (k c) -> p k c", k=KC)
            pooled = AB[:, 0:KC * NT0].rearrange("p (k t) -> p k t", k=KC)
            outs = []
            if pas == 0:
                dve(lambda e: e.tensor_copy(out=halo[:, jp, :, :], in_=hT[:, :, M + TB - 15:M + TB]))
                t_st = em.dma(HB[:, :, 0:15], pstate_in[jp].rearrange("(k p) r -> p k r", p=128))
                for kc in range(KC):
                    dve(lambda e, kc=kc: e.scalar_tensor_tensor(out=HB[:, kc, 15:19], in0=xT[:, kc, TB:TB + NS], scalar=V(vmix, kc),
                                                                in1=rstd[:, TB:TB + NS], op0=ALU.mult, op1=ALU.mult), [t_st])
                outs.append(em.dma(ps_out[jp].rearrange("(k p) r -> p k r", p=128), HB[:, :, 4:19], deps=[em.last["dve"]]))
            else:
                dve(lambda e: e.tensor_copy(out=hT[:, :, 1:M], in_=halo[:, jp, :, :]))
                for kc in range(KC):
                    dve(lambda e, kc=kc: e.scalar_tensor_tensor(out=P32[:, kc, :], in0=xT[:, kc, TB - 15:TB], scalar=V(vmix, kc),
                                                                in1=rstd[:, TB - 15:TB], op0=ALU.mult, op1=ALU.mult))
                outs.append(em.dma(pp_out[jp].rearrange("(k p) r -> p k r", p=128), P32[:, :, :], deps=[em.last["dve"]]))

            def pool_cols(a, ch, Cn, t0, dst, gi, fix, prev):
                w = 2 ** (gi + 1)
                tk = dve(lambda e: e.tensor_tensor(out=T1[:, ch + 1:Cn], in0=a[:, ch + 1:Cn], in1=a[:, ch:Cn - 1], op=ALU.add), [prev])
                cur, oth, lo, sh = T1, T2, ch + 1, 2
                while sh < w:
                    lo += sh
                    tk = dve(lambda e, cur=cur, oth=oth, lo=lo, sh=sh: e.tensor_tensor(
                        out=oth[:, lo:Cn], in0=cur[:, lo:Cn], in1=cur[:, lo - sh:Cn - sh], op=ALU.add), [tk])
                    cur, oth = oth, cur
                    sh *= 2
                n = Cn - t0
                tk = dve(lambda e, cur=cur: e.scalar_tensor_tensor(out=dst[:, 0:n], in0=cur[:, t0:Cn], scalar=1.0 / w, in1=a[:, t0:Cn],
                                                                   op0=ALU.mult, op1=ALU.subtract), [tk])
                if fix:
                    tk = dve(lambda e, cur=cur: e.tensor_tensor(out=TS[:, 0:15], in0=cur[:, t0:t0 + 15],
                                                                in1=C_INVC[:, gi * 16:gi * 16 + 15], op=ALU.mult), [tk])
                    tk = dve(lambda e: e.tensor_tensor(out=dst[:, 0:15], in0=TS[:, 0:15], in1=a[:, t0:t0 + 15], op=ALU.subtract), [tk])
                return tk

            tk = em.last["dve"]
            for kc in range(KC):
                gi = kc // KG
                tk = pool_cols(hT[:, kc, 0:C], 1, C, M, pooled[:, kc, 0:TB], gi, pas == 0, tk)
                if pas == 0:
                    tk = pool_cols(HB[:, kc, :], 0, 19, 15, pooled[:, kc, TB:TB + NS], gi, False, tk)
            accrot = Rot([bank(4), bank(5)])
            for gi in range(4):
                slot, wtok, li = ring.get(pw_in[jp, gi].rearrange("(a p) d -> p a d", p=128),
                                          lambda s: s[:, 0:KG * PG].rearrange("p (a d) -> p a d", a=KG))
                tl = None
                for mo in range(KG):
                    for (c0, w) in tiles(pas):
                        ci, pa, clast = accrot.next()
                        tl = mm_group(pa[:, :w], [(slot[:, a, mo * 128:(mo + 1) * 128], pooled[:, gi * KG + a, c0:c0 + w]) for a in range(KG)],
                                      deps=[wtok, tk] + clast)
                        tx = dve(lambda e, pa=pa, m=gi * KG + mo, c0=c0, w=w: e.scalar_tensor_tensor(
                            out=xT[:, m, c0:c0 + w], in0=pa[:, :w], scalar=V(vscale, m), in1=xT[:, m, c0:c0 + w],
                            op0=ALU.mult, op1=ALU.add), [tl])
                        accrot.done(ci, [tx])
                ring.release(li, tl)
            em.barrier()

        def chunk_mixer(pas, jc, vgidx):
            G8 = KC // 8
            vg_bc = s32[:, 0:D]
            b_bc = s32[:, D:D + 1024].rearrange("p (g q) -> p g q", g=8)
            tmp = s32[:, D + 1024:D + 2048]
            vs32 = s32[:, D + 2048:2 * D + 2048]
            wsT = s16[:, 0:1024].rearrange("p (g q) -> p g q", g=8)
            vs16 = s16[:, 1024:1024 + D]
            junk = s16[:, 1024 + D:1024 + 2 * D]
            Us = s16[:, 1024 + 2 * D:1024 + 2 * D + KC * NS].rearrange("p (k s) -> p k s", k=KC)
            ss = sm32[:, 0:8]
            rs = sm32[:, 8:16]
            Vr = AB[:, 0:4 * D].rearrange("p (n f) -> p n f", n=4)
            U = AB[:, 8 * NT0:8 * NT0 + KC * 512].rearrange("p (k t) -> p k t", k=KC)
            t1 = em.dma(vg_bc, cvg_in[jc:jc + 1, :].to_broadcast([128, D]))
            t2 = em.dma(s32[:, D:D + 1024], cbs_in[jc:jc + 1, :].to_broadcast([128, 1024]))
            t3 = em.dma(tmp, cws_in[jc])
            tw = dve(lambda e: e.tensor_tensor(out=wsT, in0=tmp.rearrange("p (g q) -> p g q", g=8),
                                               in1=C_TRI.unsqueeze(1).to_broadcast([128, 8, 128]), op=ALU.mult), [t1, t2, t3])
            mmrot = Rot([bank(0), bank(1), bank(2), bank(3)])
            accrot = Rot([bank(4), bank(5)])
            auxrot = Rot([bank(6), bank(7)])
            tmprot = Rot([tmp[:, 0:512], tmp[:, 512:1024]])
            segs = [(0, 512), (512, 512)]
            for si, (s0, sw) in enumerate(segs):
                samp = (pas == 0 and si == 0)
                tg = None
                for ft in range(D // 512):
                    sl = []
                    for kh in range(KC // 8):
                        slot, wtok, li = ring.get(cwin_v[jc].rearrange("(k p) n -> p k n", p=128)[:, kh * 8:(kh + 1) * 8, ft * 512:(ft + 1) * 512],
                                                  lambda s: s.rearrange("p (k n) -> p k n", k=8))
                        sl.append((slot, wtok, li))
                    tp = None
                    for n in range(4):
                        pi, ps, plast = mmrot.next()
                        tp = mm_group(ps[:, :], [(hT[:, kc, M + s0 + n * 128:M + s0 + (n + 1) * 128], sl[kc // 8][0][:, kc % 8, :]) for kc in range(KC)],
                                      deps=[x[1] for x in sl] + plast + [tw])
                        tg = act(lambda e, ps=ps, n=n, ft=ft: e.activation(out=Vr[:, n, ft * 512:(ft + 1) * 512], in_=ps[:, :], func=AF.Gelu), [tp])
                        mmrot.done(pi, [tg])
                    if samp:
                        pi, ps, plast = mmrot.next()
                        tp = mm_group(ps[0:NS, :], [(hT[:, kc, M + TB:M + TB + NS], sl[kc // 8][0][:, kc % 8, :]) for kc in range(KC)],
                                      deps=plast)
                        tgs = act(lambda e, ps=ps, ft=ft: e.activation(out=vs32[0:NS, ft * 512:(ft + 1) * 512], in_=ps[0:NS, :], func=AF.Gelu), [tp])
                        mmrot.done(pi, [tgs])
                    for (_, _, li) in sl:
                        ring.release(li, tp)
                tn = None
                for n in range(4):
                    ta = act(lambda e, n=n: e.activation(out=junk, in_=Vr[:, n, :], func=AF.Square, accum_out=ss[:, n:n + 1]), [tn])
                    td = act(lambda e, n=n: e.activation(out=rs[:, n:n + 1], in_=ss[:, n:n + 1], func=AF.Sqrt, scale=1.0 / D, bias=EPS), [ta])
                    td = dve(lambda e, n=n: e.reciprocal(out=rs[:, n:n + 1], in_=rs[:, n:n + 1]), [td])
                    tn = dve(lambda e, n=n: e.scalar_tensor_tensor(out=Vr[:, n, :], in0=Vr[:, n, :], scalar=rs[:, n:n + 1], in1=vg_bc,
                                                                   op0=ALU.mult, op1=ALU.mult), [td])
                if samp:
                    ta = act(lambda e: e.activation(out=junk[0:NS, :], in_=vs32[0:NS, :], func=AF.Square, accum_out=ss[0:NS, 4:5]), [tn])
                    td = act(lambda e: e.activation(out=rs[0:NS, 4:5], in_=ss[0:NS, 4:5], func=AF.Sqrt, scale=1.0 / D, bias=EPS), [ta])
                    td = dve(lambda e: e.reciprocal(out=rs[0:NS, 4:5], in_=rs[0:NS, 4:5]), [td])
                    td = dve(lambda e: e.scalar_tensor_tensor(out=vs32[0:NS, :], in0=vs32[0:NS, :], scalar=rs[0:NS, 4:5], in1=vg_bc[0:NS, :],
                                                              op0=ALU.mult, op1=ALU.mult), [td])
                    tn = dve(lambda e: e.tensor_copy(out=vs16[0:NS, :], in_=vs32[0:NS, :]), [td])
                    em.dma(cv_out[jc], vs32[0:NS, :], deps=[td])
                tgl = None
                for m in range(KC):
                    g = m // G8
                    slot, wtok, li = ring.get(cwin_u[jc, m], lambda s: s[:, 0:KC * 128].rearrange("p (k c) -> p k c", k=KC))
                    pi, ps, plast = mmrot.next()
                    tp = mm_group(ps[:, :], [(slot[:, kc, :], hT[:, kc, M + s0:M + s0 + 512]) for kc in range(KC)], deps=[wtok] + plast)
                    tu = act(lambda e, ps=ps, m=m: e.activation(out=U[:, m, :], in_=ps[:, :], func=AF.Gelu), [tp, tgl])
                    mmrot.done(pi, [tu])
                    if samp:
                        pi, ps2, plast = mmrot.next()
                        tp = mm_group(ps2[:, 0:NS], [(slot[:, kc, :], hT[:, kc, M + TB:M + TB + NS]) for kc in range(KC)], deps=plast)
                        tus = act(lambda e, ps2=ps2, m=m: e.activation(out=Us[:, m, :], in_=ps2[:, 0:NS], func=AF.Gelu), [tp])
                        mmrot.done(pi, [tus])
                    ring.release(li, tp)
                    xi, px, xlast = auxrot.next()
                    tmx = None
                    for n in range(4):
                        tmx = mm_group(px[:, n * 128:(n + 1) * 128], [(Vr[:, n, m * 128:(m + 1) * 128], wsT[:, g, :])],
                                       deps=[tn] + xlast if n == 0 else (), sig=(n == 3))
                    ti, tt, tlast = tmprot.next()
                    td = None
                    for n in range(4):
                        td = dve(lambda e, px=px, tt=tt, n=n, g=g: e.tensor_tensor(out=tt[:, n * 128:(n + 1) * 128], in0=px[:, n * 128:(n + 1) * 128],
                                                                                 in1=b_bc[:, g, :], op=ALU.add), [tmx] + tlast)
                    tgl = dve(lambda e, m=m, tt=tt: e.tensor_tensor(out=U[:, m, :], in0=U[:, m, :], in1=tt, op=ALU.mult), [td, tu])
                    auxrot.done(xi, [td])
                    tmprot.done(ti, [tgl])
                    if samp:
                        xi, px, xlast = auxrot.next()
                        tmx = mm_group(px[:, 0:NS], [(vs16[0:NS, m * 128:(m + 1) * 128], wsT[0:NS, g, 0:NS])], deps=[tn] + xlast)
                        td = dve(lambda e, px=px, g=g: e.tensor_tensor(out=sm32[:, 16:20], in0=px[:, 0:NS], in1=b_bc[:, g, 0:NS], op=ALU.add), [tmx])
                        tgl = dve(lambda e, m=m: e.tensor_tensor(out=Us[:, m, :], in0=Us[:, m, :], in1=sm32[:, 16:20], op=ALU.mult), [td, tus])
                        auxrot.done(xi, [td])
                for mo in range(KC):
                    slot, wtok, li = ring.get(cwo[jc, mo], lambda s: s[:, 0:KC * 128].rearrange("p (k c) -> p k c", k=KC))
                    ci, pa, clast = accrot.next()
                    tl = mm_group(pa[:, :], [(slot[:, m, :], U[:, m, :]) for m in range(KC)], deps=[wtok, tgl] + clast)
                    tx = dve(lambda e, pa=pa, mo=mo, s0=s0: e.tensor_tensor(out=xT[:, mo, s0:s0 + 512], in0=pa[:, :], in1=xT[:, mo, s0:s0 + 512],
                                                                            op=ALU.add), [tl])
                    accrot.done(ci, [tx])
                    if samp:
                        ci, pa, clast = accrot.next()
                        tl = mm_group(pa[:, 0:NS], [(slot[:, m, :], Us[:, m, :]) for m in range(KC)], deps=clast)
                        tx = dve(lambda e, pa=pa, mo=mo: e.tensor_tensor(out=xT[:, mo, TB:TB + NS], in0=pa[:, 0:NS], in1=xT[:, mo, TB:TB + NS],
                                                                         op=ALU.add), [tl])
                        accrot.done(ci, [tx])
                    ring.release(li, tl)
                em.barrier()

        def attn_mixer(pas, ja):
            NT = TB + (NS if pas == 0 else 0)
            SC = float(128 ** -0.5)
            o = 0
            cosT = s32[:, o:o + NT0]; o += NT0
            sinT = s32[:, o:o + NT0]; o += NT0
            qgrot = Rot([s32[:, o:o + 512], s32[:, o + 512:o + 1024]]); o += 1024
            rsrot = Rot([s32[:, o:o + 512], s32[:, o + 512:o + 1024]]); o += 1024
            t1b = s32[:, o:o + 512]; o += 512
            t2b = s32[:, o:o + 512]; o += 512
            kstrot = Rot([s32[:, o:o + 512]]); o += 512
            vstrot = Rot([s32[:, o + i * 128:o + (i + 1) * 128] for i in range(4)]); o += 512
            assert o <= S32, o
            b = 0
            qT = s16[:, b:b + NT0]; b += NT0
            KT = s16[:, b:b + SEQ + NS]; b += SEQ + NS
            VB = s16[:, b:b + 2048].rearrange("p (n e) -> p n e", n=16); b += 2048
            brot = Rot([s16[:, b + i * 512:b + (i + 1) * 512] for i in range(4)]); b += 2048
            sqrot = brot
            ptrot = brot
            assert b <= S16, b
            OT = AB[:, 0:H * NT0].rearrange("p (h t) -> p h t", h=H)
            mmrot = Rot([bank(0), bank(1), bank(2), bank(3)])
            auxrot = mmrot
            NUM = lambda qb: PS[:, 4 + qb // 4, (qb % 4) * 128:(qb % 4 + 1) * 128]
            DEN = lambda qb: PS[:, 6 + qb // 4, (qb % 4) * 128:(qb % 4 + 1) * 128]
            tr = [em.dma(cosT[:, 0:TB], rope_in[:, 0, pas * TB:(pas + 1) * TB]), em.dma(sinT[:, 0:TB], rope_in[:, 1, pas * TB:(pas + 1) * TB])]
            if pas == 0:
                tr.append(em.dma(cosT[:, TB:TB + NS], rope_in[:, 0, SEQ:SEQ + NS]))
                tr.append(em.dma(sinT[:, TB:TB + NS], rope_in[:, 1, SEQ:SEQ + NS]))
            samp_last = []
            def odma(out, in_, deps=()):
                if "nokvout" in DBG:
                    return None
                return em.dma(out, in_, deps=deps)
            fin_tok = []
            kt_free = []
            outd = []
            for h in range(H):
                for bk in (4, 5, 6, 7):
                    mm_group(PS[:, bk, :], [(zer[:, :], hT[:, 0, M:M + 512])], deps=fin_tok if bk == 4 else (), sig=(bk == 7))
                for g in range(3):
                    ks = SEQ - KEEP[g]
                    att_reads = []
                    wr_toks = []
                    def qk_A(which, c0, w, slot, wtok):
                        pi, ps, plast = mmrot.next()
                        tq = mm_group(ps[:, :w], [(slot[:, kc, :], hT[:, kc, M + c0:M + c0 + w]) for kc in range(KC)], deps=[wtok] + plast)
                        si, sq, slast = sqrot.next()
                        ta1 = act(lambda e: e.activation(out=sq[:, :w], in_=ps[:, :w], func=AF.Square), [tq] + slast)
                        qi, qg, qlast = qgrot.next()
                        ta2 = act(lambda e: e.activation(out=qg[:, :w], in_=ps[:, :w], func=AF.Identity,
                                                         scale=agn[:, ja, which:which + 1]), qlast)
                        mmrot.done(pi, [ta2])
                        return (which, c0, w, si, sq, ta1, qi, qg, ta2), tq

                    def qk_B(stt):
                        which, c0, w, si, sq, ta1, qi, qg, ta2 = stt
                        xi, pss, xlast = auxrot.next()
                        tp1 = mm_group(pss[:, :w], [(ones[:, :], sq[:, :w])], deps=[ta1] + xlast)
                        sqrot.done(si, [tp1])
                        xj, psw, xlast2 = auxrot.next()
                        tp2 = mm_group(psw[:, :w], [(C_PSW, qg[:, :w])], deps=[ta2] + xlast2)
                        ri, rs_, rlast = rsrot.next()
                        ta3 = act(lambda e: e.activation(out=rs_[:, :w], in_=pss[:, :w], func=AF.Ln, scale=1.0 / 128, bias=EPS), [tp1] + rlast)
                        ta4 = act(lambda e: e.activation(out=rs_[:, :w], in_=rs_[:, :w], func=AF.Exp, scale=-0.5), [ta3])
                        auxrot.done(xi, [ta3])
                        td1 = dve(lambda e: e.tensor_tensor(out=t1b[:, :w], in0=qg[:, :w], in1=cosT[:, c0:c0 + w], op=ALU.mult), [ta2] + tr)
                        td2 = dve(lambda e: e.tensor_tensor(out=t2b[:, :w], in0=psw[:, :w], in1=sinT[:, c0:c0 + w], op=ALU.mult), [tp2])
                        auxrot.done(xj, [td2])
                        td3 = dve(lambda e: e.tensor_tensor(out=t1b[:, :w], in0=t1b[:, :w], in1=t2b[:, :w], op=ALU.add), [td2])
                        qgrot.done(qi, [td3, tp2])
                        if which == 0:
                            dst = qT[:, c0:c0 + w] if c0 < TB else smallb[:, 0, g, :]
                            td4 = dve(lambda e: e.tensor_tensor(out=dst, in0=t1b[:, :w], in1=rs_[:, :w], op=ALU.mult), [ta4] + kt_free + samp_last)
                            rsrot.done(ri, [td4])
                            wr_toks.append(td4)
                        else:
                            ki, kst, klast = kstrot.next()
                            td4 = dve(lambda e: e.tensor_tensor(out=kst[:, :w], in0=t1b[:, :w], in1=rs_[:, :w], op=ALU.mult), [ta4] + klast)
                            rsrot.done(ri, [td4])
                            dst = KT[:, pas * TB + c0:pas * TB + c0 + w] if c0 < TB else smallb[:, 1, g, :]
                            td5 = dve(lambda e: e.tensor_copy(out=dst, in_=kst[:, :w]), [td4] + kt_free + samp_last)
                            wr_toks.append(td5)
                            rd = [td5]
                            if c0 >= TB:
                                rd.append(odma(kos[g][ja, h], kst[:, 0:NS], deps=[td4]))
                            else:
                                tg0 = pas * TB + c0
                                lo = max(tg0, ks)
                                if pas == 0:
                                    t_ = odma(kscr[g, h][:, c0:c0 + w], kst[:, 0:w], deps=[td4])
                                    rd.append(t_)
                                    if t_ is not None:
                                        scr_w[(g, h)].append(t_)
                                if lo < tg0 + w:
                                    rd.append(odma(ko[g][ja, h][:, lo - ks:tg0 + w - ks], kst[:, lo - tg0:w], deps=[td4]))
                            kstrot.done(ki, rd)

                    pend = None
                    for which in (0, 1):
                        slot, wtok, li = ring.get(aqkv[ja, which * 3 * H + g * H + h], lambda s: s[:, 0:KC * 128].rearrange("p (k c) -> p k c", k=KC))
                        tq = None
                        for (c0, w) in atiles(pas):
                            stt, tq = qk_A(which, c0, w, slot, wtok)
                            if pend is not None:
                                qk_B(pend)
                            pend = stt
                        ring.release(li, tq)
                    qk_B(pend)
                    slot, wtok, li = ring.get(aqkv[ja, 2 * 3 * H + g * H + h], lambda s: s[:, 0:KC * 128].rearrange("p (k c) -> p k c", k=KC))
                    tv = None
                    for tb in range(8 if "nov" not in DBG else 0):
                        pi, ps, plast = mmrot.next()
                        tv = mm_group(ps[:, 0:128], [(hT[:, kc, M + tb * 128:M + (tb + 1) * 128], slot[:, kc, :]) for kc in range(KC)],
                                      deps=[wtok] + plast)
                        vi, vst, vlast = vstrot.next()
                        ta = act(lambda e, vst=vst, ps=ps: e.activation(out=vst, in_=ps[:, 0:128], func=AF.Identity), [tv] + vlast)
                        td = dve(lambda e, vst=vst, tb=tb: e.tensor_copy(out=VB[:, pas * 8 + tb, :], in_=vst), [ta] + kt_free)
                        wr_toks.append(td)
                        mmrot.done(pi, [ta])
                        rd = [td]
                        tg0 = pas * TB + tb * 128
                        if tg0 >= ks:
                            rd.append(odma(vo[g][ja, tg0 - ks:tg0 - ks + 128, h, :], vst, deps=[ta]))
                        if pas == 0:
                            t_ = odma(vscr[g, h][:, tb * 128:(tb + 1) * 128], vst, deps=[ta])
                            rd.append(t_)
                            if t_ is not None:
                                scr_w[(g, h)].append(t_)
                        vstrot.done(vi, rd)
                    if pas == 0 and "nov" not in DBG and "v_nosamp" not in DBG:
                        pi, ps, plast = mmrot.next()
                        tv = mm_group(ps[0:NS, 0:128], [(hT[:, kc, M + TB:M + TB + NS], slot[:, kc, :]) for kc in range(KC)], deps=plast)
                        vi, vst, vlast = vstrot.next()
                        ta = act(lambda e, vst=vst, ps=ps: e.activation(out=vst[0:NS, :], in_=ps[0:NS, 0:128], func=AF.Identity), [tv] + vlast)
                        td = None
                        if "v_nodve" not in DBG:
                            td = dve(lambda e, vst=vst, g=g: e.tensor_copy(out=vsb[0:NS, g, :], in_=vst[0:NS, :]), [ta] + samp_last)
                        mmrot.done(pi, [ta])
                        vstrot.done(vi, [td, odma(vos[g][ja, :, h, :], vst[0:NS, :], deps=[ta])])
                    ring.release(li, tv if tv is not None else em.last["pe"])
                    prev = None
                    if pas == 1 and "noprev" not in DBG:
                        npv = {0: 1, 1: 4, 2: 8}[g]
                        kpv, kptok, kpli = ring.get(kscr[g, h][:, (8 - npv) * 128:TB], lambda s, npv=npv: s[:, 0:npv * 128], deps=scr_w[(g, h)])
                        vpv, vptok, vpli = ring.get(vscr[g, h][:, (8 - npv) * 128:TB],
                                                    lambda s, npv=npv: s[:, 0:npv * 128].rearrange("p (n e) -> p n e", n=npv), deps=scr_w[(g, h)])
                        prev = (8 - npv, kpv, vpv)
                        wr_toks += [kptok, vptok]
                    def at_S(qb, b0, bt, nkb):
                        nb = len(bt)
                        mb = MASK_BASE[g]
                        mi = mb[0] if b0 == 0 else (mb[1] if g == 1 else 8)
                        xi, pS, xlast = auxrot.next()
                        tS = None
                        for i, kb in enumerate(bt):
                            kap = KT[:, kb * 128:(kb + 1) * 128] if (pas == 0 or kb >= 8) else prev[1][:, (kb - prev[0]) * 128:(kb - prev[0] + 1) * 128]
                            tS = mm_group(pS[:, i * 128:(i + 1) * 128], [(kap, qT[:, qb * 128:(qb + 1) * 128])],
                                          deps=(wr_toks + xlast) if i == 0 else (), sig=(i == nb - 1))
                        pi_, PT, ptlast = ptrot.next()
                        ta = act(lambda e: e.activation(out=PT[:, 0:nb * 128], in_=pS[:, 0:nb * 128], func=AF.Exp, scale=SC), [tS] + ptlast)
                        auxrot.done(xi, [ta])
                        td = dve(lambda e: e.tensor_tensor(out=PT[:, 0:nb * 128], in0=PT[:, 0:nb * 128],
                                                           in1=masks[:, mi:mi + nb, :].rearrange("p a b -> p (a b)"), op=ALU.mult), [ta])
                        return (qb, b0, bt, nkb, pi_, PT, td)

                    def at_PV(stt):
                        qb, b0, bt, nkb, pi_, PT, td = stt
                        nb = len(bt)
                        t_ = None
                        for i, kb in enumerate(bt):
                            last = (g == 2 and b0 + 4 >= nkb and i == nb - 1)
                            vap = VB[:, kb, :] if (pas == 0 or kb >= 8) else prev[2][:, kb - prev[0], :]
                            mm_group(NUM(qb), [(vap, PT[:, i * 128:(i + 1) * 128])], deps=[td], sig=False, flags=(False, last))
                            t_ = mm_group(DEN(qb), [(ones[:, :], PT[:, i * 128:(i + 1) * 128])], sig=(i == nb - 1), flags=(False, last))
                        ptrot.done(pi_, [t_])
                        return t_

                    tpv = None
                    pendq = []
                    for qb in range(8):
                        qbg = pas * 8 + qb
                        kbs = [qbg - dl for dl in range(NDEL[g]) if qbg - dl >= 0]
                        for b0 in range(0, len(kbs), 4):
                            pendq.append(at_S(qb, b0, kbs[b0:b0 + 4], len(kbs)))
                            if len(pendq) > 2:
                                tpv = at_PV(pendq.pop(0))
                    while pendq:
                        tpv = at_PV(pendq.pop(0))
                    if tpv is None:
                        tpv = em.last["pe"]
                    att_reads.append(tpv)
                    kt_free = att_reads
                    if pas == 1 and "noprev" not in DBG:
                        ring.release(kpli, tpv)
                        ring.release(vpli, tpv)
                fin_tok = []
                for hf in range(2 if "noattn" not in DBG else 0):
                    td = dve(lambda e, hf=hf: e.reciprocal(out=t2b[:, :], in_=PS[:, 6 + hf, :]), [tpv])
                    td = dve(lambda e, hf=hf, h=h: e.tensor_tensor(out=OT[:, h, hf * 512:(hf + 1) * 512], in0=PS[:, 4 + hf, :], in1=t2b[:, :], op=ALU.mult), [td])
                    fin_tok = [td]
                if pas == 0 and "nosamp" not in DBG:
                    NUMS = PS[:, 4, 0:NS]
                    DENS = PS[:, 4, NS:2 * NS]
                    tlast = None
                    mm_group(PS[:, 4, 0:2 * NS], [(zer[:, :], hT[:, 0, M:M + 2 * NS])], deps=fin_tok)
                    for g in range(3):
                        L = WINS[g]
                        nblk = L // 128
                        kcs, kct, kli = ring.get(kc_in[g][ja, h], lambda s, L=L: s[:, 0:L])
                        vcs, vct, vli = ring.get(vc_in[g][ja, h], lambda s, L=L, nblk=nblk: s[:, 0:L].rearrange("p (n e) -> p n e", n=nblk))
                        xi, pS, xlast = auxrot.next()
                        tS = None
                        for kb in range(nblk):
                            tS = mm_group(pS[:, kb * NS:(kb + 1) * NS], [(kcs[:, kb * 128:(kb + 1) * 128], smallb[:, 0, g, :])],
                                          deps=[kct, vct] + xlast + fin_tok if kb == 0 else (), sig=(kb == nblk - 1))
                        pi_, PT, ptlast = ptrot.next()
                        ta = act(lambda e, PT=PT, pS=pS, nblk=nblk: e.activation(out=PT[:, 0:nblk * NS], in_=pS[:, 0:nblk * NS], func=AF.Exp, scale=SC),
                                 [tS] + ptlast)
                        auxrot.done(xi, [ta])
                        td = dve(lambda e, PT=PT, nblk=nblk, g=g: e.tensor_tensor(out=PT[:, 0:nblk * NS], in0=PT[:, 0:nblk * NS],
                                                                                 in1=smaskb[:, g, 0:nblk * NS], op=ALU.mult), [ta])
                        for kb in range(nblk):
                            first = False
                            mm_group(NUMS, [(vcs[:, kb, :], PT[:, kb * NS:(kb + 1) * NS])], deps=[td], sig=False, flags=(first, False))
                            tlast = mm_group(DENS, [(ones[:, :], PT[:, kb * NS:(kb + 1) * NS])], sig=(kb == nblk - 1), flags=(first, False))
                        ptrot.done(pi_, [tlast])
                        ring.release(kli, tlast)
                        ring.release(vli, tlast)
                        xi, pS, xlast = auxrot.next()
                        tS = mm_group(pS[0:NS, 0:NS], [(smallb[:, 1, g, :], smallb[:, 0, g, :])], deps=xlast)
                        pi_, PT, ptlast = ptrot.next()
                        ta = act(lambda e, PT=PT, pS=pS: e.activation(out=PT[0:NS, 0:NS], in_=pS[0:NS, 0:NS], func=AF.Exp, scale=SC), [tS] + ptlast)
                        auxrot.done(xi, [ta])
                        td = dve(lambda e, PT=PT, g=g: e.tensor_tensor(out=PT[0:NS, 0:NS], in0=PT[0:NS, 0:NS], in1=nmaskb[0:NS, g, :], op=ALU.mult), [ta])
                        lastg = (g == 2)
                        mm_group(NUMS, [(vsb[0:NS, g, :], PT[0:NS, 0:NS])], deps=[td], sig=False, flags=(False, lastg))
                        tlast = mm_group(DENS, [(ones[0:NS, :], PT[0:NS, 0:NS])], flags=(False, lastg))
                        ptrot.done(pi_, [tlast])
                    td = dve(lambda e: e.reciprocal(out=sm32[:, 24:28], in_=DENS), [tlast])
                    td = dve(lambda e, h=h: e.tensor_tensor(out=OT[:, h, TB:TB + NS], in0=NUMS, in1=sm32[:, 24:28], op=ALU.mult), [td])
                    fin_tok = [td]
                    samp_last = [tlast]
            em.barrier()
            accrot = Rot([bank(4), bank(5)])
            for mo in range(KC):
                slot, wtok, li = ring.get(awo[ja, mo], lambda s: s[:, 0:H * 128].rearrange("p (k c) -> p k c", k=H))
                tl = None
                for (c0, w) in tiles(pas):
                    ci, pa, clast = accrot.next()
                    tl = mm_group(pa[:, :w], [(slot[:, hh, :], OT[:, hh, c0:c0 + w]) for hh in range(H)], deps=[wtok] + clast)
                    tx = dve(lambda e, pa=pa, mo=mo, c0=c0, w=w: e.tensor_tensor(out=xT[:, mo, c0:c0 + w], in0=pa[:, :w], in1=xT[:, mo, c0:c0 + w],
                                                                                op=ALU.add), [tl])
                    accrot.done(ci, [tx])
                ring.release(li, tl)
            em.barrier()

        scr_w = {(g, h): [] for g in range(3) for h in range(H)}
        stage = [0]

        def stop():
            stage[0] += 1
            return upto is not None and stage[0] > upto

        for pas in range(2):
            xv = x_in.rearrange("(k p) t -> p k t", p=128)
            for a in range(0, KC, 4):
                em.dma(xT[:, a:a + 4, 0:TB], xv[:, a:a + 4, pas * TB:(pas + 1) * TB])
            if pas == 0:
                em.dma(xT[:, :, TB:TB + NS], xs_in.rearrange("(k p) s -> p k s", p=128))
            em.barrier()
            done = False
            for l in range(cfg.DEPTH):
                kind, j = l % 3, l // 3
                if stop():
                    done = True
                    break
                rmsnorm(pas, l)
                ffn(pas, w1i[l], w1o[l])
                if stop():
                    done = True
                    break
                rmsnorm(pas, cfg.DEPTH + l)
                if kind == 0:
                    pool_mixer(pas, j, cfg.DEPTH + l, 3 * cfg.DEPTH + j)
                elif kind == 1:
                    chunk_mixer(pas, j, 3 * cfg.DEPTH + NPL + j)
                else:
                    attn_mixer(pas, j)
                if stop():
                    done = True
                    break
                rmsnorm(pas, 2 * cfg.DEPTH + l)
                ffn(pas, w2i[l], w2o[l])
            stage[0] = 0
            yv = y_out.rearrange("(k p) t -> p k t", p=128)
            for a in range(0, KC, 4):
                em.dma(yv[:, a:a + 4, pas * TB:(pas + 1) * TB], xT[:, a:a + 4, 0:TB])
            if pas == 0:
                em.dma(ys_out.rearrange("(k p) s -> p k s", p=128), xT[:, :, TB:TB + NS])
            em.barrier()
        ring.finalize()
        final = em.all_dma_toks()

        def semof(key):
            kind, i = key
            if kind == "e":
                return esem[i]
            if kind == "d":
                return dsem[i]
            return wsem[i]

        def run(eng, name):
            for fn, waits, sig in em.ops[name]:
                for key, val in waits:
                    eng.wait_ge(semof(key), val)
                if fn is None:
                    continue
                r = fn(eng)
                if isinstance(r, tuple) and r[0] == "dma":
                    _, o, s_, k = r
                    eng.dma_start(out=o, in_=s_).then_inc(semof(k), 16)
                elif sig:
                    r.then_inc(esem[name], 1)
            if name == "sp":
                for key, val in final:
                    eng.wait_ge(semof(key), val)

        @block.tensor
        def _(t):
            run(t, "pe")

        @block.scalar
        def _(s):
            run(s, "act")

        @block.vector
        def _(v):
            run(v, "dve")

        @block.gpsimd
        def _(g):
            run(g, "pool")

        @block.sync
        def _(sp):
            run(sp, "sp")
    return nc


def coltile(W):
    K, N = W.shape
    return np.ascontiguousarray(W.reshape(K // 128, 128, N // 128, 128).transpose(2, 1, 0, 3))


def make_consts():
    c = np.zeros((128, 320), np.float32)
    cm = np.zeros((128, 1740), np.float32)
    k = np.arange(128)[:, None]
    q = np.arange(128)[None, :]
    c[:, 0:128] = (k == (q + 64) % 128)
    for gi, w in enumerate((2, 4, 8, 16)):
        c[:, 128 + gi * 16:128 + gi * 16 + 16] = 1.0 / np.minimum(w, np.arange(16) + 1)
    c[:, 192:320] = (k <= q)
    ms = []
    ms.append(q >= k)
    ms.append(q <= k)
    m4 = ((q - k) % 4 == 0)
    ms += [m4 & (q >= k), m4, m4, m4, m4 & (q <= k)]
    m16 = ((q - k) % 16 == 0)
    ms += [m16 & (q >= k), m16, m16, m16, m16]
    cm[:, 0:1536] = np.concatenate([m.astype(np.float32) for m in ms], axis=1)
    i = np.arange(128)[:, None]
    t = np.arange(4)[None, :]
    sm = [np.tile((i >= t), (1, 16)), np.tile((i % 4 == t), (1, 16)), np.tile((i % 16 == t), (1, 16))]
    cm[:, 1536:1536 + 192] = np.concatenate([m.astype(np.float32) for m in sm], axis=1)
    kk = np.arange(4)[:, None]
    nm = [(kk <= t), (kk == t), (kk == t)]
    cm[0:4, 1728:1740] = np.concatenate([m.astype(np.float32) for m in nm], axis=1)
    return c, cm


def make_rope():
    half = 64
    freqs = (np.float32(10000.0) ** (-2.0 * np.arange(half, dtype=np.float32) / np.float32(128))).astype(np.float32)
    pos = np.concatenate([np.arange(SEQ), PAST + np.arange(NS)]).astype(np.float32)
    ang = (pos[None, :] * freqs[:, None]).astype(np.float32)
    cos = np.cos(ang).astype(np.float32)
    sin = np.sin(ang).astype(np.float32)
    r = np.zeros((128, 2, SEQ + NS), np.float32)
    r[:64, 0] = cos
    r[64:, 0] = cos
    r[:64, 1] = -sin
    r[64:, 1] = sin
    return r


def prep_inputs(cfg, inp):
    D, KC, NJ, H = cfg.D, cfg.KC, cfg.NJ, cfg.H
    f = lambda a: np.ascontiguousarray(np.asarray(a, dtype=np.float32))
    sh = {}
    vl = [inp["norm_ffn1"], inp["norm_mix"], inp["norm_ffn2"], inp["pool_scale"], inp["chunk_v_norm"]]
    allv = np.concatenate([f(v) for v in vl], axis=0)
    sh["vecs"] = np.ascontiguousarray(allv.reshape(cfg.NV, KC, 128).transpose(2, 0, 1))
    sh["consts"], sh["cmask"] = make_consts()
    sh["rope"] = make_rope()

    def win(w):
        w = f(w)
        out = np.empty((cfg.DEPTH, NJ, 128, 2, KC, 128), np.float32)
        for l in range(cfg.DEPTH):
            ct = coltile(w[l])
            out[l, :, :, 0] = ct[:NJ]
            out[l, :, :, 1] = ct[NJ:]
        return out.reshape(cfg.DEPTH, NJ, 128, 2 * KC * 128)

    sh["w1i"] = win(inp["ffn1_w_in"])
    sh["w2i"] = win(inp["ffn2_w_in"])
    sh["w1o"] = f(inp["ffn1_w_out"]).reshape(cfg.DEPTH, NJ, 128, D)
    sh["w2o"] = f(inp["ffn2_w_out"]).reshape(cfg.DEPTH, NJ, 128, D)
    sh["pool_w"] = f(inp["pool_w"])
    cw = f(inp["chunk_w_in"])
    sh["cw_v"] = np.ascontiguousarray(cw[:, :, D:])
    sh["cw_u"] = np.stack([coltile(cw[j, :, :D]).reshape(KC, 128, KC * 128) for j in range(cfg.NCL)])
    sh["cw_o"] = np.stack([coltile(f(inp["chunk_w_out"])[j]).reshape(KC, 128, KC * 128) for j in range(cfg.NCL)])
    sh["c_vg"] = f(inp["chunk_v_norm"])
    sh["c_ws"] = np.ascontiguousarray(f(inp["chunk_w_s"]).transpose(0, 3, 1, 2)).reshape(cfg.NCL, 128, 8 * 128)
    sh["c_bs"] = f(inp["chunk_b_s"]).reshape(cfg.NCL, 8 * 128)
    sh["a_qkv"] = np.stack([coltile(f(inp["attn_w_qkv"])[j]).reshape(9 * H, 128, KC * 128) for j in range(cfg.NAL)])
    sh["a_wo"] = np.stack([coltile(f(inp["attn_w_out"])[j]).reshape(KC, 128, H * 128) for j in range(cfg.NAL)])
    sh["a_gn"] = np.ascontiguousarray(np.stack([f(inp["attn_q_norm"]), f(inp["attn_k_norm"])], axis=-1))
    xp = f(inp["x_prompt"])
    xs = f(inp["x_sample"])
    st = f(inp["state_pool"])
    caches = [f(inp["cache_kv_g0"]), f(inp["cache_kv_g1"]), f(inp["cache_kv_g2"])]
    maps = []
    zx = np.zeros((D, SEQ), np.float32)
    for c in range(cfg.NCORES):
        m = dict(sh)
        m["x_in"] = np.ascontiguousarray(xp[PCORE.index(c)].T) if c in PCORE else zx
        m["xs_in"] = np.ascontiguousarray(xs[c].T)
        m["pool_state"] = np.ascontiguousarray(st[:, c].transpose(0, 2, 1))
        for g in range(3):
            ck = caches[g][:, c]
            m["kcache%d" % g] = np.ascontiguousarray(ck[:, :, 0].transpose(0, 2, 3, 1))
            L = ck.shape[1]
            v = ck[:, :, 1].reshape(cfg.NAL, L // 128, 128, H, 128).transpose(0, 3, 2, 1, 4)
            m["vcache%d" % g] = np.ascontiguousarray(v).reshape(cfg.NAL, H, 128, L)
        maps.append(m)
    return maps


def assemble(cfg, res, nb=4):
    H = cfg.H
    R = lambda c, k: np.asarray(res[c][k], dtype=np.float32)
    DB = cfg.NCORES
    PC = PCORE
    y_p = np.stack([R(PC[b], "y").T for b in range(nb)])
    y_s = np.stack([R(c, "ys").T for c in range(DB)])
    pool_p = np.stack([R(PC[b], "pool_p").transpose(0, 2, 1) for b in range(nb)], axis=1)
    pool_s = np.stack([R(c, "pool_s").transpose(0, 2, 1) for c in range(DB)], axis=1)
    chunk_v = np.stack([R(c, "chunk_v") for c in range(DB)], axis=1)
    outs = [y_p, y_s, pool_p, pool_s, chunk_v]
    for g in range(3):
        kp = np.stack([R(PC[b], "ko%d" % g).transpose(0, 3, 1, 2) for b in range(nb)], axis=1)
        vp = np.stack([R(PC[b], "vo%d" % g) for b in range(nb)], axis=1)
        outs.append(np.ascontiguousarray(np.stack([kp, vp], axis=3)))
        ksm = np.stack([R(c, "kos%d" % g).transpose(0, 3, 1, 2) for c in range(DB)], axis=1)
        vsm = np.stack([R(c, "vos%d" % g) for c in range(DB)], axis=1)
        outs.append(np.ascontiguousarray(np.stack([ksm, vsm], axis=3)))
    return tuple(outs)


def kernel(**inputs):
    cfg = Cfg()
    maps = prep_inputs(cfg, inputs)
    nc = build(cfg)
    res = run_bass_kernel_spmd(nc, maps, core_ids=list(range(cfg.NCORES)))
    return assemble(cfg, res.results)
```

```python
import numpy as np
import concourse.bass as bass
import concourse.mybir as mybir
from concourse.bass_utils import run_bass_kernel_spmd

F32 = mybir.dt.float32
BF16 = mybir.dt.bfloat16
AF = mybir.ActivationFunctionType
ALU = mybir.AluOpType

EPS = 1e-6
M = 16
TB = 1024
NS = 4
SEQ = 2048
PAST = 16384
NSLOT = 4
SLOTW = 4096
ND = 24
WINS = (128, 512, 2048)
DILS = (1, 4, 16)
NDEL = (2, 5, 16)
KEEP = (128, 512, 2048)
MASK_BASE = {0: (0, None), 1: (2, 6), 2: (7, 8)}
DBG = set()
PCORE = (0, 1, 4, 5)


class Cfg:
    def __init__(self, D=2048, DFF=5504, H=16, DEPTH=4, NCORES=8):
        self.D, self.DFF, self.H, self.DEPTH, self.NCORES = D, DFF, H, DEPTH, NCORES
        self.KC = D // 128
        self.NJ = DFF // 128
        self.AW = 3 * H * 128
        self.NPL = (DEPTH + 2) // 3
        self.NCL = (DEPTH + 1) // 3
        self.NAL = DEPTH // 3
        self.NV = 3 * DEPTH + self.NPL + self.NCL


ENG = ("pe", "act", "dve", "pool", "sp")


class Em:
    def __init__(self):
        self.ops = {e: [] for e in ENG}
        self.cnt = {e: 0 for e in ENG}
        self.seen = {e: {} for e in ENG}
        self.last = {e: None for e in ENG}
        self.ndma = 0
        self.dval = [0] * ND

    def op(self, eng, fn, deps=(), sig=True):
        waits = []
        for d in deps:
            if d is None:
                continue
            key, val = d
            if self.seen[eng].get(key, 0) >= val:
                continue
            self.seen[eng][key] = val
            waits.append((key, val))
        tok = None
        if sig:
            self.cnt[eng] += 1
            tok = (("e", eng), self.cnt[eng])
            self.last[eng] = tok
        self.ops[eng].append((fn, waits, sig))
        return tok

    def dma(self, out, in_, deps=(), q="sp"):
        i = self.ndma % ND
        self.ndma += 1
        key = ("d", i)
        prev = self.dval[i]
        self.dval[i] += 16
        deps = list(deps)
        if prev:
            deps.append((key, prev))
        val = self.dval[i]
        self.op(q, lambda e, o=out, s=in_, k=key: ("dma", o, s, k), deps=deps, sig=False)
        return (key, val)

    def all_dma_toks(self):
        return [(("d", i), v) for i, v in enumerate(self.dval) if v]

    def barrier(self):
        toks = [self.last[e] for e in ("pe", "act", "dve")] + self.all_dma_toks()
        for e in ("pe", "act", "dve", "sp"):
            for d in toks:
                if d is None:
                    continue
                key, val = d
                if self.seen[e].get(key, 0) >= val:
                    continue
                self.seen[e][key] = val
                self.ops[e].append((None, [(key, val)], False))


class Rot:
    def __init__(self, aps):
        self.aps = list(aps)
        self.last = [[] for _ in self.aps]
        self.i = 0

    def next(self):
        i = self.i % len(self.aps)
        self.i += 1
        return i, self.aps[i], list(self.last[i])

    def done(self, i, toks):
        self.last[i] = [t for t in toks if t is not None]


class Ring:
    def __init__(self, em, slots):
        self.em, self.slots = em, slots
        self.loads, self.rel = [], []

    def get(self, src, view, deps=()):
        i = len(self.loads)
        slot = i % NSLOT
        dst = view(self.slots[slot])
        self.loads.append((src, dst, slot, list(deps)))
        self.rel.append(None)
        return dst, (("w", slot), 16 * (i // NSLOT + 1)), i

    def release(self, i, tok):
        assert tok is not None
        self.rel[i] = tok

    def finalize(self):
        for i, (src, dst, slot, xd) in enumerate(self.loads):
            assert self.rel[i] is not None, i
            deps = xd + ([self.rel[i - NSLOT]] if i >= NSLOT else [])
            self.em.op("pool", lambda e, o=dst, s=src, k=("w", slot): ("dma", o, s, k), deps=deps, sig=False)


def build(cfg, upto=None):
    D, KC, NJ, H = cfg.D, cfg.KC, cfg.NJ, cfg.H
    NPL, NCL, NAL = cfg.NPL, cfg.NCL, cfg.NAL
    nc = bass.Bass("TRN2", target_bir_lowering=False)

    def din(name, shape):
        return nc.dram_tensor(name, list(shape), F32, kind="ExternalInput").ap()

    def dout(name, shape):
        return nc.dram_tensor(name, list(shape), F32, kind="ExternalOutput").ap()

    x_in = din("x_in", [D, SEQ])
    xs_in = din("xs_in", [D, NS])
    vecs_in = din("vecs", [128, cfg.NV, KC])
    consts_in = din("consts", [128, 320])
    cmask_in = din("cmask", [128, 1740])
    rope_in = din("rope", [128, 2, SEQ + NS])
    w1i = din("w1i", [cfg.DEPTH, NJ, 128, 2 * KC * 128])
    w1o = din("w1o", [cfg.DEPTH, NJ, 128, D])
    w2i = din("w2i", [cfg.DEPTH, NJ, 128, 2 * KC * 128])
    w2o = din("w2o", [cfg.DEPTH, NJ, 128, D])
    pw_in = din("pool_w", [NPL, 4, D // 4, D // 4])
    pstate_in = din("pool_state", [NPL, D, 15])
    cwin_v = din("cw_v", [NCL, D, D])
    cwin_u = din("cw_u", [NCL, KC, 128, KC * 128])
    cwo = din("cw_o", [NCL, KC, 128, KC * 128])
    cvg_in = din("c_vg", [NCL, D])
    cws_in = din("c_ws", [NCL, 128, 8 * 128])
    cbs_in = din("c_bs", [NCL, 8 * 128])
    aqkv = din("a_qkv", [NAL, 9 * H, 128, KC * 128])
    awo = din("a_wo", [NAL, KC, 128, H * 128])
    agn_in = din("a_gn", [NAL, 128, 2])
    kc_in = [din("kcache%d" % g, [NAL, H, 128, WINS[g]]) for g in range(3)]
    vc_in = [din("vcache%d" % g, [NAL, H, 128, WINS[g]]) for g in range(3)]

    y_out = dout("y", [D, SEQ])
    ys_out = dout("ys", [D, NS])
    pp_out = dout("pool_p", [NPL, D, 15])
    ps_out = dout("pool_s", [NPL, D, 15])
    cv_out = dout("chunk_v", [NCL, NS, D])
    ko = [dout("ko%d" % g, [NAL, H, 128, KEEP[g]]) for g in range(3)]
    vo = [dout("vo%d" % g, [NAL, KEEP[g], H, 128]) for g in range(3)]
    kos = [dout("kos%d" % g, [NAL, H, 128, NS]) for g in range(3)]
    vos = [dout("vos%d" % g, [NAL, NS, H, 128]) for g in range(3)]
    kscr = dout("kscr", [3, H, 128, TB])
    vscr = dout("vscr", [3, H, 128, 8 * 128])

    NT0 = TB + NS
    em = Em()
    S32 = 6160
    S16 = 7176
    import contextlib
    with contextlib.ExitStack() as st:
        xT = st.enter_context(nc.sbuf_tensor("xT", [128, KC, NT0], F32))
        hT = st.enter_context(nc.sbuf_tensor("hT", [128, KC, M + NT0], BF16))
        AB = st.enter_context(nc.sbuf_tensor("AB", [128, 16 * NT0], BF16))
        RG = st.enter_context(nc.sbuf_tensor("RG", [128, NSLOT, SLOTW], BF16))
        s32 = st.enter_context(nc.sbuf_tensor("s32", [128, S32], F32))
        s16 = st.enter_context(nc.sbuf_tensor("s16", [128, S16], BF16))
        vsb2 = st.enter_context(nc.sbuf_tensor("vsb2", [128, 3 * 128], BF16))
        vsb = vsb2[:, :].rearrange("p (g e) -> p g e", g=3)
        vecs = st.enter_context(nc.sbuf_tensor("vecs_sb", [128, cfg.NV, KC], F32))
        cst = st.enter_context(nc.sbuf_tensor("cst", [128, 320], F32))
        ones = st.enter_context(nc.sbuf_tensor("ones", [128, 128], BF16))
        zer = st.enter_context(nc.sbuf_tensor("zer", [128, 128], BF16))
        masks = st.enter_context(nc.sbuf_tensor("masks", [128, 12, 128], BF16))
        smaskb = st.enter_context(nc.sbuf_tensor("smaskb", [128, 3, 64], BF16))
        nmaskb = st.enter_context(nc.sbuf_tensor("nmaskb", [128, 3, 4], BF16))
        halo = st.enter_context(nc.sbuf_tensor("halo", [128, max(NPL, 1), KC, 15], BF16))
        agn = st.enter_context(nc.sbuf_tensor("agn", [128, max(NAL, 1), 2], F32))
        smallb = st.enter_context(nc.sbuf_tensor("smallb", [128, 3, 3, 4], BF16))
        sm32 = st.enter_context(nc.sbuf_tensor("sm32", [128, 40], F32))
        PS = st.enter_context(nc.psum_tensor("PS", [128, 8, 512], F32))
        esem = {e: st.enter_context(nc.semaphore("sem_" + e)) for e in ("pe", "act", "dve")}
        dsem = [st.enter_context(nc.semaphore("dsem%d" % i)) for i in range(ND)]
        wsem = [st.enter_context(nc.semaphore("wsem%d" % i)) for i in range(NSLOT)]
        block = st.enter_context(nc.Block())

        ring = Ring(em, [RG[:, i, :] for i in range(NSLOT)])
        C_PSW = cst[:, 0:128]
        C_INVC = cst[:, 128:192]
        C_TRI = cst[:, 192:320]
        C_MASK = s32[:, 0:1536]
        C_SM = s32[:, 1536:1536 + 192]
        C_NM = s32[:, 1728:1728 + 12]

        def V(idx, kc):
            return vecs[:, idx, kc:kc + 1]

        def act(fn, deps=()):
            return em.op("act", fn, deps)

        def dve(fn, deps=()):
            return em.op("dve", fn, deps)

        def mm_group(out, pairs, deps=(), sig=True, flags=None):
            n = len(pairs)
            tok = None
            for i, (l, r) in enumerate(pairs):
                s0 = (i == 0) if flags is None else flags[0] and i == 0
                s1 = (i == n - 1) if flags is None else flags[1] and i == n - 1
                tok = em.op("pe", lambda t, o=out, l=l, r=r, a=s0, b=s1, sk=(flags is not None): t.matmul(o, l, r, start=a, stop=b, skip_group_check=sk),
                            deps=deps if i == 0 else (), sig=(sig and i == n - 1))
            return tok

        def bank(i):
            return PS[:, i, :]

        def tiles(pas):
            if pas == 0:
                return [(0, 343), (343, 343), (686, 342)]
            return [(0, 512), (512, 512)]

        def atiles(pas):
            t = [(0, 512), (512, 512)]
            if pas == 0:
                t.append((TB, NS))
            return t

        em.dma(cst[:, :], consts_in[:, :])
        t_c = em.dma(s32[:, 0:1740], cmask_in[:, :])
        t_v = em.dma(vecs[:, :, :], vecs_in[:, :, :])
        if NAL:
            t_g = em.dma(agn[:, :, :], agn_in.rearrange("l p t -> p l t"))
        dve(lambda e: e.memset(ones[:, :], 1.0))
        dve(lambda e: e.memset(zer[:, :], 0.0))
        dve(lambda e: e.tensor_copy(out=masks[:, :, :], in_=C_MASK.rearrange("p (a b) -> p a b", a=12)), [t_c])
        dve(lambda e: e.tensor_copy(out=smaskb[:, :, :], in_=C_SM.rearrange("p (a b) -> p a b", a=3)), [t_c])
        dve(lambda e: e.tensor_copy(out=nmaskb[:, :, :], in_=C_NM.rearrange("p (a b) -> p a b", a=3)), [t_c])
        dve(lambda e: e.memset(hT[:, :, 0:M], 0.0))
        em.barrier()

        rstd = s32[:, 0:NT0]
        FS = 1032

        def rmsnorm(pas, vidx):
            NT = TB + (NS if pas == 0 else 0)
            ta = None
            tb_ = None
            for kc in range(KC):
                if kc % 2 == 0:
                    ta = act(lambda e, kc=kc: e.activation(out=hT[:, kc, M:M + NT], in_=xT[:, kc, 0:NT], func=AF.Square))
                else:
                    tb_ = dve(lambda e, kc=kc: e.tensor_tensor(out=hT[:, kc, M:M + NT], in0=xT[:, kc, 0:NT], in1=xT[:, kc, 0:NT], op=ALU.mult))
            rot = Rot([bank(6), bank(7)])
            tds = []
            for (c0, w) in tiles(pas):
                i, ps, lst = rot.next()
                tp = mm_group(ps[:, :w], [(ones[:, :], hT[:, kc, M + c0:M + c0 + w]) for kc in range(KC)], deps=[ta, tb_] + lst)
                td = act(lambda e, ps=ps, c0=c0, w=w: e.activation(out=rstd[:, c0:c0 + w], in_=ps[:, :w], func=AF.Sqrt,
                                                                    scale=1.0 / D, bias=EPS), [tp])
                td = dve(lambda e, c0=c0, w=w: e.reciprocal(out=rstd[:, c0:c0 + w], in_=rstd[:, c0:c0 + w]), [td])
                rot.done(i, [td])
                tds.append(td)
                tpl = tp
            for kc in range(KC):
                dve(lambda e, kc=kc: e.scalar_tensor_tensor(out=hT[:, kc, M:M + NT], in0=xT[:, kc, 0:NT], scalar=V(vidx, kc),
                                                            in1=rstd[:, 0:NT], op0=ALU.mult, op1=ALU.mult), [tpl] + tds)
            em.barrier()

        def ffn(pas, wi, wo):
            NT = TB + (NS if pas == 0 else 0)
            GS = 6
            groups = [list(range(a, min(a + GS, NJ))) for a in range(0, NJ, GS)]
            abrot = Rot([AB[:, 0:8 * NT0].rearrange("p (a t) -> p a t", a=8), AB[:, 8 * NT0:16 * NT0].rearrange("p (a t) -> p a t", a=8)])
            mmrot = Rot([(bank(0), bank(1)), (bank(2), bank(3))])
            accrot = Rot([bank(4), bank(5)])
            sgrot = Rot([s32[:, FS:FS + 512], s32[:, FS + 512:FS + 1024]])
            def win_phase(grp):
                ai, ab, alast = abrot.next()
                tdl = None
                for jj, j in enumerate(grp):
                    slot, wtok, li = ring.get(wi[j], lambda s: s[:, 0:2 * KC * 128].rearrange("p (a k m) -> p a k m", a=2, k=KC))
                    tu = None
                    for (c0, w) in tiles(pas):
                        pi, (pg, pu), plast = mmrot.next()
                        mm_group(pg[:, :w], [(slot[:, 0, kc, :], hT[:, kc, M + c0:M + c0 + w]) for kc in range(KC)],
                                 deps=[wtok] + plast, sig=False)
                        tu = mm_group(pu[:, :w], [(slot[:, 1, kc, :], hT[:, kc, M + c0:M + c0 + w]) for kc in range(KC)])
                        si, sg, slast = sgrot.next()
                        ta = act(lambda e, sg=sg, pg=pg, w=w: e.activation(out=sg[:, :w], in_=pg[:, :w], func=AF.Silu), [tu] + slast)
                        tdl = dve(lambda e, ab=ab, jj=jj, c0=c0, w=w, sg=sg, pu=pu: e.tensor_tensor(
                            out=ab[:, jj, c0:c0 + w], in0=sg[:, :w], in1=pu[:, :w], op=ALU.mult), [ta] + alast)
                        mmrot.done(pi, [tdl])
                        sgrot.done(si, [tdl])
                    ring.release(li, tu)
                return (grp, ai, ab, tdl)

            def wout_phase(stt):
                grp, ai, ab, tdl = stt
                wsl = []
                for a in range(0, len(grp), 2):
                    n = min(2, len(grp) - a)
                    j0 = grp[a]
                    slot, wtok, li = ring.get(wo[j0:j0 + n].rearrange("a p n -> p a n"),
                                              lambda s, n=n: s[:, 0:n * D].rearrange("p (a n) -> p a n", a=n))
                    wsl.append((slot, wtok, li))
                tl = None
                for m in range(KC):
                    for (c0, w) in tiles(pas):
                        ci, pa, clast = accrot.next()
                        tl = mm_group(pa[:, :w], [(wsl[jj // 2][0][:, jj % 2, m * 128:(m + 1) * 128], ab[:, jj, c0:c0 + w])
                                                  for jj in range(len(grp))],
                                      deps=[x[1] for x in wsl] + [tdl] + clast)
                        tx = dve(lambda e, pa=pa, m=m, c0=c0, w=w: e.scalar_tensor_tensor(
                            out=xT[:, m, c0:c0 + w], in0=pa[:, :w], scalar=0.5, in1=xT[:, m, c0:c0 + w],
                            op0=ALU.mult, op1=ALU.add), [tl])
                        accrot.done(ci, [tx])
                for (_, _, li) in wsl:
                    ring.release(li, tl)
                abrot.done(ai, [tl])

            pend = None
            for grp in groups:
                stt = win_phase(grp)
                if pend is not None:
                    wout_phase(pend)
                pend = stt
            wout_phase(pend)
            em.barrier()

        def pool_mixer(pas, jp, vmix, vscale):
            NT = TB + (NS if pas == 0 else 0)
            KG = KC // 4
            PG = D // 4
            C = M + TB
            T1 = s32[:, FS:FS + C]
            T2 = s32[:, FS + C:FS + 2 * C]
            HB = s32[:, FS + 2 * C:FS + 2 * C + KC * 19].rearrange("p (k c) -> p k c", k=KC)
            TS = s32[:, FS + 2 * C + KC * 19:FS + 2 * C + KC * 19 + 64]
            P32 = s32[:, FS + 2 * C + KC * 19 + 64:FS + 2 * C + KC * 19 + 64 + KC * 15].rearrange("p (k c) -> p k c", k=KC)
            pooled = AB[:, 0:KC * NT0].rearrange("p (k t) -> p k t", k=KC)
            outs = []
            if pas == 0:
                dve(lambda e: e.tensor_copy(out=halo[:, jp, :, :], in_=hT[:, :, M + TB - 15:M + TB]))
                t_st = em.dma(HB[:, :, 0:15], pstate_in[jp].rearrange("(k p) r -> p k r", p=128))
                for kc in range(KC):
                    dve(lambda e, kc=kc: e.scalar_tensor_tensor(out=HB[:, kc, 15:19], in0=xT[:, kc, TB:TB + NS], scalar=V(vmix, kc),
                                                                in1=rstd[:, TB:TB + NS], op0=ALU.mult, op1=ALU.mult), [t_st])
                outs.append(em.dma(ps_out[jp].rearrange("(k p) r -> p k r", p=128), HB[:, :, 4:19], deps=[em.last["dve"]]))
            else:
                dve(lambda e: e.tensor_copy(out=hT[:, :, 1:M], in_=halo[:, jp, :, :]))
                for kc in range(KC):
                    dve(lambda e, kc=kc: e.scalar_tensor_tensor(out=P32[:, kc, :], in0=xT[:, kc, TB - 15:TB], scalar=V(vmix, kc),
                                                                in1=rstd[:, TB - 15:TB], op0=ALU.mult, op1=ALU.mult))
                outs.append(em.dma(pp_out[jp].rearrange("(k p) r -> p k r", p=128), P32[:, :, :], deps=[em.last["dve"]]))

            def pool_cols(a, ch, Cn, t0, dst, gi, fix, prev):
                w = 2 ** (gi + 1)
                tk = dve(lambda e: e.tensor_tensor(out=T1[:, ch + 1:Cn], in0=a[:, ch + 1:Cn], in1=a[:, ch:Cn - 1], op=ALU.add), [prev])
                cur, oth, lo, sh = T1, T2, ch + 1, 2
                while sh < w:
                    lo += sh
                    tk = dve(lambda e, cur=cur, oth=oth, lo=lo, sh=sh: e.tensor_tensor(
                        out=oth[:, lo:Cn], in0=cur[:, lo:Cn], in1=cur[:, lo - sh:Cn - sh], op=ALU.add), [tk])
                    cur, oth = oth, cur
                    sh *= 2
                n = Cn - t0
                tk = dve(lambda e, cur=cur: e.scalar_tensor_tensor(out=dst[:, 0:n], in0=cur[:, t0:Cn], scalar=1.0 / w, in1=a[:, t0:Cn],
                                                                   op0=ALU.mult, op1=ALU.subtract), [tk])
                if fix:
                    tk = dve(lambda e, cur=cur: e.tensor_tensor(out=TS[:, 0:15], in0=cur[:, t0:t0 + 15],
                                                                in1=C_INVC[:, gi * 16:gi * 16 + 15], op=ALU.mult), [tk])
                    tk = dve(lambda e: e.tensor_tensor(out=dst[:, 0:15], in0=TS[:, 0:15], in1=a[:, t0:t0 + 15], op=ALU.subtract), [tk])
                return tk

            tk = em.last["dve"]
            for kc in range(KC):
                gi = kc // KG
                tk = pool_cols(hT[:, kc, 0:C], 1, C, M, pooled[:, kc, 0:TB], gi, pas == 0, tk)
                if pas == 0:
                    tk = pool_cols(HB[:, kc, :], 0, 19, 15, pooled[:, kc, TB:TB + NS], gi, False, tk)
            accrot = Rot([bank(4), bank(5)])
            for gi in range(4):
                slot, wtok, li = ring.get(pw_in[jp, gi].rearrange("(a p) d -> p a d", p=128),
                                          lambda s: s[:, 0:KG * PG].rearrange("p (a d) -> p a d", a=KG))
                tl = None
                for mo in range(KG):
                    for (c0, w) in tiles(pas):
                        ci, pa, clast = accrot.next()
                        tl = mm_group(pa[:, :w], [(slot[:, a, mo * 128:(mo + 1) * 128], pooled[:, gi * KG + a, c0:c0 + w]) for a in range(KG)],
                                      deps=[wtok, tk] + clast)
                        tx = dve(lambda e, pa=pa, m=gi * KG + mo, c0=c0, w=w: e.scalar_tensor_tensor(
                            out=xT[:, m, c0:c0 + w], in0=pa[:, :w], scalar=V(vscale, m), in1=xT[:, m, c0:c0 + w],
                            op0=ALU.mult, op1=ALU.add), [tl])
                        accrot.done(ci, [tx])
                ring.release(li, tl)
            em.barrier()

        def chunk_mixer(pas, jc, vgidx):
            G8 = KC // 8
            vg_bc = s32[:, 0:D]
            b_bc = s32[:, D:D + 1024].rearrange("p (g q) -> p g q", g=8)
            tmp = s32[:, D + 1024:D + 2048]
            vs32 = s32[:, D + 2048:2 * D + 2048]
            wsT = s16[:, 0:1024].rearrange("p (g q) -> p g q", g=8)
            vs16 = s16[:, 1024:1024 + D]
            junk = s16[:, 1024 + D:1024 + 2 * D]
            Us = s16[:, 1024 + 2 * D:1024 + 2 * D + KC * NS].rearrange("p (k s) -> p k s", k=KC)
            ss = sm32[:, 0:8]
            rs = sm32[:, 8:16]
            Vr = AB[:, 0:4 * D].rearrange("p (n f) -> p n f", n=4)
            U = AB[:, 8 * NT0:8 * NT0 + KC * 512].rearrange("p (k t) -> p k t", k=KC)
            t1 = em.dma(vg_bc, cvg_in[jc:jc + 1, :].to_broadcast([128, D]))
            t2 = em.dma(s32[:, D:D + 1024], cbs_in[jc:jc + 1, :].to_broadcast([128, 1024]))
            t3 = em.dma(tmp, cws_in[jc])
            tw = dve(lambda e: e.tensor_tensor(out=wsT, in0=tmp.rearrange("p (g q) -> p g q", g=8),
                                               in1=C_TRI.unsqueeze(1).to_broadcast([128, 8, 128]), op=ALU.mult), [t1, t2, t3])
            mmrot = Rot([bank(0), bank(1), bank(2), bank(3)])
            accrot = Rot([bank(4), bank(5)])
            auxrot = Rot([bank(6), bank(7)])
            tmprot = Rot([tmp[:, 0:512], tmp[:, 512:1024]])
            segs = [(0, 512), (512, 512)]
            for si, (s0, sw) in enumerate(segs):
                samp = (pas == 0 and si == 0)
                tg = None
                for ft in range(D // 512):
                    sl = []
                    for kh in range(KC // 8):
                        slot, wtok, li = ring.get(cwin_v[jc].rearrange("(k p) n -> p k n", p=128)[:, kh * 8:(kh + 1) * 8, ft * 512:(ft + 1) * 512],
                                                  lambda s: s.rearrange("p (k n) -> p k n", k=8))
                        sl.append((slot, wtok, li))
                    tp = None
                    for n in range(4):
                        pi, ps, plast = mmrot.next()
                        tp = mm_group(ps[:, :], [(hT[:, kc, M + s0 + n * 128:M + s0 + (n + 1) * 128], sl[kc // 8][0][:, kc % 8, :]) for kc in range(KC)],
                                      deps=[x[1] for x in sl] + plast + [tw])
                        tg = act(lambda e, ps=ps, n=n, ft=ft: e.activation(out=Vr[:, n, ft * 512:(ft + 1) * 512], in_=ps[:, :], func=AF.Gelu), [tp])
                        mmrot.done(pi, [tg])
                    if samp:
                        pi, ps, plast = mmrot.next()
                        tp = mm_group(ps[0:NS, :], [(hT[:, kc, M + TB:M + TB + NS], sl[kc // 8][0][:, kc % 8, :]) for kc in range(KC)],
                                      deps=plast)
                        tgs = act(lambda e, ps=ps, ft=ft: e.activation(out=vs32[0:NS, ft * 512:(ft + 1) * 512], in_=ps[0:NS, :], func=AF.Gelu), [tp])
                        mmrot.done(pi, [tgs])
                    for (_, _, li) in sl:
                        ring.release(li, tp)
                tn = None
                for n in range(4):
                    ta = act(lambda e, n=n: e.activation(out=junk, in_=Vr[:, n, :], func=AF.Square, accum_out=ss[:, n:n + 1]), [tn])
                    td = act(lambda e, n=n: e.activation(out=rs[:, n:n + 1], in_=ss[:, n:n + 1], func=AF.Sqrt, scale=1.0 / D, bias=EPS), [ta])
                    td = dve(lambda e, n=n: e.reciprocal(out=rs[:, n:n + 1], in_=rs[:, n:n + 1]), [td])
                    tn = dve(lambda e, n=n: e.scalar_tensor_tensor(out=Vr[:, n, :], in0=Vr[:, n, :], scalar=rs[:, n:n + 1], in1=vg_bc,
                                                                   op0=ALU.mult, op1=ALU.mult), [td])
                if samp:
                    ta = act(lambda e: e.activation(out=junk[0:NS, :], in_=vs32[0:NS, :], func=AF.Square, accum_out=ss[0:NS, 4:5]), [tn])
                    td = act(lambda e: e.activation(out=rs[0:NS, 4:5], in_=ss[0:NS, 4:5], func=AF.Sqrt, scale=1.0 / D, bias=EPS), [ta])
                    td = dve(lambda e: e.reciprocal(out=rs[0:NS, 4:5], in_=rs[0:NS, 4:5]), [td])
                    td = dve(lambda e: e.scalar_tensor_tensor(out=vs32[0:NS, :], in0=vs32[0:NS, :], scalar=rs[0:NS, 4:5], in1=vg_bc[0:NS, :],
                                                              op0=ALU.mult, op1=ALU.mult), [td])
                    tn = dve(lambda e: e.tensor_copy(out=vs16[0:NS, :], in_=vs32[0:NS, :]), [td])
                    em.dma(cv_out[jc], vs32[0:NS, :], deps=[td])
                tgl = None
                for m in range(KC):
                    g = m // G8
                    slot, wtok, li = ring.get(cwin_u[jc, m], lambda s: s[:, 0:KC * 128].rearrange("p (k c) -> p k c", k=KC))
                    pi, ps, plast = mmrot.next()
                    tp = mm_group(ps[:, :], [(slot[:, kc, :], hT[:, kc, M + s0:M + s0 + 512]) for kc in range(KC)], deps=[wtok] + plast)
                    tu = act(lambda e, ps=ps, m=m: e.activation(out=U[:, m, :], in_=ps[:, :], func=AF.Gelu), [tp, tgl])
                    mmrot.done(pi, [tu])
                    if samp:
                        pi, ps2, plast = mmrot.next()
                        tp = mm_group(ps2[:, 0:NS], [(slot[:, kc, :], hT[:, kc, M + TB:M + TB + NS]) for kc in range(KC)], deps=plast)
                        tus = act(lambda e, ps2=ps2, m=m: e.activation(out=Us[:, m, :], in_=ps2[:, 0:NS], func=AF.Gelu), [tp])
                        mmrot.done(pi, [tus])
                    ring.release(li, tp)
                    xi, px, xlast = auxrot.next()
                    tmx = None
                    for n in range(4):
                        tmx = mm_group(px[:, n * 128:(n + 1) * 128], [(Vr[:, n, m * 128:(m + 1) * 128], wsT[:, g, :])],
                                       deps=[tn] + xlast if n == 0 else (), sig=(n == 3))
                    ti, tt, tlast = tmprot.next()
                    td = None
                    for n in range(4):
                        td = dve(lambda e, px=px, tt=tt, n=n, g=g: e.tensor_tensor(out=tt[:, n * 128:(n + 1) * 128], in0=px[:, n * 128:(n + 1) * 128],
                                                                                 in1=b_bc[:, g, :], op=ALU.add), [tmx] + tlast)
                    tgl = dve(lambda e, m=m, tt=tt: e.tensor_tensor(out=U[:, m, :], in0=U[:, m, :], in1=tt, op=ALU.mult), [td, tu])
                    auxrot.done(xi, [td])
                    tmprot.done(ti, [tgl])
                    if samp:
                        xi, px, xlast = auxrot.next()
                        tmx = mm_group(px[:, 0:NS], [(vs16[0:NS, m * 128:(m + 1) * 128], wsT[0:NS, g, 0:NS])], deps=[tn] + xlast)
                        td = dve(lambda e, px=px, g=g: e.tensor_tensor(out=sm32[:, 16:20], in0=px[:, 0:NS], in1=b_bc[:, g, 0:NS], op=ALU.add), [tmx])
                        tgl = dve(lambda e, m=m: e.tensor_tensor(out=Us[:, m, :], in0=Us[:, m, :], in1=sm32[:, 16:20], op=ALU.mult), [td, tus])
                        auxrot.done(xi, [td])
                for mo in range(KC):
                    slot, wtok, li = ring.get(cwo[jc, mo], lambda s: s[:, 0:KC * 128].rearrange("p (k c) -> p k c", k=KC))
                    ci, pa, clast = accrot.next()
                    tl = mm_group(pa[:, :], [(slot[:, m, :], U[:, m, :]) for m in range(KC)], deps=[wtok, tgl] + clast)
                    tx = dve(lambda e, pa=pa, mo=mo, s0=s0: e.tensor_tensor(out=xT[:, mo, s0:s0 + 512], in0=pa[:, :], in1=xT[:, mo, s0:s0 + 512],
                                                                            op=ALU.add), [tl])
                    accrot.done(ci, [tx])
                    if samp:
                        ci, pa, clast = accrot.next()
                        tl = mm_group(pa[:, 0:NS], [(slot[:, m, :], Us[:, m, :]) for m in range(KC)], deps=clast)
                        tx = dve(lambda e, pa=pa, mo=mo: e.tensor_tensor(out=xT[:, mo, TB:TB + NS], in0=pa[:, 0:NS], in1=xT[:, mo, TB:TB + NS],
                                                                         op=ALU.add), [tl])
                        accrot.done(ci, [tx])
                    ring.release(li, tl)
                em.barrier()

        def attn_mixer(pas, ja):
            NT = TB + (NS if pas == 0 else 0)
            SC = float(128 ** -0.5)
            o = 0
            cosT = s32[:, o:o + NT0]; o += NT0
            sinT = s32[:, o:o + NT0]; o += NT0
            qgrot = Rot([s32[:, o:o + 512], s32[:, o + 512:o + 1024]]); o += 1024
            rsrot = Rot([s32[:, o:o + 512], s32[:, o + 512:o + 1024]]); o += 1024
            t1b = s32[:, o:o + 512]; o += 512
            t2b = s32[:, o:o + 512]; o += 512
            kstrot = Rot([s32[:, o:o + 512]]); o += 512
            vstrot = Rot([s32[:, o + i * 128:o + (i + 1) * 128] for i in range(4)]); o += 512
            assert o <= S32, o
            b = 0
            qT = s16[:, b:b + NT0]; b += NT0
            KT = s16[:, b:b + SEQ + NS]; b += SEQ + NS
            VB = s16[:, b:b + 2048].rearrange("p (n e) -> p n e", n=16); b += 2048
            brot = Rot([s16[:, b + i * 512:b + (i + 1) * 512] for i in range(4)]); b += 2048
            sqrot = brot
            ptrot = brot
            assert b <= S16, b
            OT = AB[:, 0:H * NT0].rearrange("p (h t) -> p h t", h=H)
            mmrot = Rot([bank(0), bank(1), bank(2), bank(3)])
            auxrot = mmrot
            NUM = lambda qb: PS[:, 4 + qb // 4, (qb % 4) * 128:(qb % 4 + 1) * 128]
            DEN = lambda qb: PS[:, 6 + qb // 4, (qb % 4) * 128:(qb % 4 + 1) * 128]
            tr = [em.dma(cosT[:, 0:TB], rope_in[:, 0, pas * TB:(pas + 1) * TB]), em.dma(sinT[:, 0:TB], rope_in[:, 1, pas * TB:(pas + 1) * TB])]
            if pas == 0:
                tr.append(em.dma(cosT[:, TB:TB + NS], rope_in[:, 0, SEQ:SEQ + NS]))
                tr.append(em.dma(sinT[:, TB:TB + NS], rope_in[:, 1, SEQ:SEQ + NS]))
            samp_last = []
            def odma(out, in_, deps=()):
                if "nokvout" in DBG:
                    return None
                return em.dma(out, in_, deps=deps)
            fin_tok = []
            kt_free = []
            outd = []
            for h in range(H):
                for bk in (4, 5, 6, 7):
                    mm_group(PS[:, bk, :], [(zer[:, :], hT[:, 0, M:M + 512])], deps=fin_tok if bk == 4 else (), sig=(bk == 7))
                for g in range(3):
                    ks = SEQ - KEEP[g]
                    att_reads = []
                    wr_toks = []
                    def qk_A(which, c0, w, slot, wtok):
                        pi, ps, plast = mmrot.next()
                        tq = mm_group(ps[:, :w], [(slot[:, kc, :], hT[:, kc, M + c0:M + c0 + w]) for kc in range(KC)], deps=[wtok] + plast)
                        si, sq, slast = sqrot.next()
                        ta1 = act(lambda e: e.activation(out=sq[:, :w], in_=ps[:, :w], func=AF.Square), [tq] + slast)
                        qi, qg, qlast = qgrot.next()
                        ta2 = act(lambda e: e.activation(out=qg[:, :w], in_=ps[:, :w], func=AF.Identity,
                                                         scale=agn[:, ja, which:which + 1]), qlast)
                        mmrot.done(pi, [ta2])
                        return (which, c0, w, si, sq, ta1, qi, qg, ta2), tq

                    def qk_B(stt):
                        which, c0, w, si, sq, ta1, qi, qg, ta2 = stt
                        xi, pss, xlast = auxrot.next()
                        tp1 = mm_group(pss[:, :w], [(ones[:, :], sq[:, :w])], deps=[ta1] + xlast)
                        sqrot.done(si, [tp1])
                        xj, psw, xlast2 = auxrot.next()
                        tp2 = mm_group(psw[:, :w], [(C_PSW, qg[:, :w])], deps=[ta2] + xlast2)
                        ri, rs_, rlast = rsrot.next()
                        ta3 = act(lambda e: e.activation(out=rs_[:, :w], in_=pss[:, :w], func=AF.Ln, scale=1.0 / 128, bias=EPS), [tp1] + rlast)
                        ta4 = act(lambda e: e.activation(out=rs_[:, :w], in_=rs_[:, :w], func=AF.Exp, scale=-0.5), [ta3])
                        auxrot.done(xi, [ta3])
                        td1 = dve(lambda e: e.tensor_tensor(out=t1b[:, :w], in0=qg[:, :w], in1=cosT[:, c0:c0 + w], op=ALU.mult), [ta2] + tr)
                        td2 = dve(lambda e: e.tensor_tensor(out=t2b[:, :w], in0=psw[:, :w], in1=sinT[:, c0:c0 + w], op=ALU.mult), [tp2])
                        auxrot.done(xj, [td2])
                        td3 = dve(lambda e: e.tensor_tensor(out=t1b[:, :w], in0=t1b[:, :w], in1=t2b[:, :w], op=ALU.add), [td2])
                        qgrot.done(qi, [td3, tp2])
                        if which == 0:
                            dst = qT[:, c0:c0 + w] if c0 < TB else smallb[:, 0, g, :]
                            td4 = dve(lambda e: e.tensor_tensor(out=dst, in0=t1b[:, :w], in1=rs_[:, :w], op=ALU.mult), [ta4] + kt_free + samp_last)
                            rsrot.done(ri, [td4])
                            wr_toks.append(td4)
                        else:
                            ki, kst, klast = kstrot.next()
                            td4 = dve(lambda e: e.tensor_tensor(out=kst[:, :w], in0=t1b[:, :w], in1=rs_[:, :w], op=ALU.mult), [ta4] + klast)
                            rsrot.done(ri, [td4])
                            dst = KT[:, pas * TB + c0:pas * TB + c0 + w] if c0 < TB else smallb[:, 1, g, :]
                            td5 = dve(lambda e: e.tensor_copy(out=dst, in_=kst[:, :w]), [td4] + kt_free + samp_last)
                            wr_toks.append(td5)
                            rd = [td5]
                            if c0 >= TB:
                                rd.append(odma(kos[g][ja, h], kst[:, 0:NS], deps=[td4]))
                            else:
                                tg0 = pas * TB + c0
                                lo = max(tg0, ks)
                                if pas == 0:
                                    t_ = odma(kscr[g, h][:, c0:c0 + w], kst[:, 0:w], deps=[td4])
                                    rd.append(t_)
                                    if t_ is not None:
                                        scr_w[(g, h)].append(t_)
                                if lo < tg0 + w:
                                    rd.append(odma(ko[g][ja, h][:, lo - ks:tg0 + w - ks], kst[:, lo - tg0:w], deps=[td4]))
                            kstrot.done(ki, rd)

                    pend = None
                    for which in (0, 1):
                        slot, wtok, li = ring.get(aqkv[ja, which * 3 * H + g * H + h], lambda s: s[:, 0:KC * 128].rearrange("p (k c) -> p k c", k=KC))
                        tq = None
                        for (c0, w) in atiles(pas):
                            stt, tq = qk_A(which, c0, w, slot, wtok)
                            if pend is not None:
                                qk_B(pend)
                            pend = stt
                        ring.release(li, tq)
                    qk_B(pend)
                    slot, wtok, li = ring.get(aqkv[ja, 2 * 3 * H + g * H + h], lambda s: s[:, 0:KC * 128].rearrange("p (k c) -> p k c", k=KC))
                    tv = None
                    for tb in range(8 if "nov" not in DBG else 0):
                        pi, ps, plast = mmrot.next()
                        tv = mm_group(ps[:, 0:128], [(hT[:, kc, M + tb * 128:M + (tb + 1) * 128], slot[:, kc, :]) for kc in range(KC)],
                                      deps=[wtok] + plast)
                        vi, vst, vlast = vstrot.next()
                        ta = act(lambda e, vst=vst, ps=ps: e.activation(out=vst, in_=ps[:, 0:128], func=AF.Identity), [tv] + vlast)
                        td = dve(lambda e, vst=vst, tb=tb: e.tensor_copy(out=VB[:, pas * 8 + tb, :], in_=vst), [ta] + kt_free)
                        wr_toks.append(td)
                        mmrot.done(pi, [ta])
                        rd = [td]
                        tg0 = pas * TB + tb * 128
                        if tg0 >= ks:
                            rd.append(odma(vo[g][ja, tg0 - ks:tg0 - ks + 128, h, :], vst, deps=[ta]))
                        if pas == 0:
                            t_ = odma(vscr[g, h][:, tb * 128:(tb + 1) * 128], vst, deps=[ta])
                            rd.append(t_)
                            if t_ is not None:
                                scr_w[(g, h)].append(t_)
                        vstrot.done(vi, rd)
                    if pas == 0 and "nov" not in DBG and "v_nosamp" not in DBG:
                        pi, ps, plast = mmrot.next()
                        tv = mm_group(ps[0:NS, 0:128], [(hT[:, kc, M + TB:M + TB + NS], slot[:, kc, :]) for kc in range(KC)], deps=plast)
                        vi, vst, vlast = vstrot.next()
                        ta = act(lambda e, vst=vst, ps=ps: e.activation(out=vst[0:NS, :], in_=ps[0:NS, 0:128], func=AF.Identity), [tv] + vlast)
                        td = None
                        if "v_nodve" not in DBG:
                            td = dve(lambda e, vst=vst, g=g: e.tensor_copy(out=vsb[0:NS, g, :], in_=vst[0:NS, :]), [ta] + samp_last)
                        mmrot.done(pi, [ta])
                        vstrot.done(vi, [td, odma(vos[g][ja, :, h, :], vst[0:NS, :], deps=[ta])])
                    ring.release(li, tv if tv is not None else em.last["pe"])
                    prev = None
                    if pas == 1 and "noprev" not in DBG:
                        npv = {0: 1, 1: 4, 2: 8}[g]
                        kpv, kptok, kpli = ring.get(kscr[g, h][:, (8 - npv) * 128:TB], lambda s, npv=npv: s[:, 0:npv * 128], deps=scr_w[(g, h)])
                        vpv, vptok, vpli = ring.get(vscr[g, h][:, (8 - npv) * 128:TB],
                                                    lambda s, npv=npv: s[:, 0:npv * 128].rearrange("p (n e) -> p n e", n=npv), deps=scr_w[(g, h)])
                        prev = (8 - npv, kpv, vpv)
                        wr_toks += [kptok, vptok]
                    def at_S(qb, b0, bt, nkb):
                        nb = len(bt)
                        mb = MASK_BASE[g]
                        mi = mb[0] if b0 == 0 else (mb[1] if g == 1 else 8)
                        xi, pS, xlast = auxrot.next()
                        tS = None
                        for i, kb in enumerate(bt):
                            kap = KT[:, kb * 128:(kb + 1) * 128] if (pas == 0 or kb >= 8) else prev[1][:, (kb - prev[0]) * 128:(kb - prev[0] + 1) * 128]
                            tS = mm_group(pS[:, i * 128:(i + 1) * 128], [(kap, qT[:, qb * 128:(qb + 1) * 128])],
                                          deps=(wr_toks + xlast) if i == 0 else (), sig=(i == nb - 1))
                        pi_, PT, ptlast = ptrot.next()
                        ta = act(lambda e: e.activation(out=PT[:, 0:nb * 128], in_=pS[:, 0:nb * 128], func=AF.Exp, scale=SC), [tS] + ptlast)
                        auxrot.done(xi, [ta])
                        td = dve(lambda e: e.tensor_tensor(out=PT[:, 0:nb * 128], in0=PT[:, 0:nb * 128],
                                                           in1=masks[:, mi:mi + nb, :].rearrange("p a b -> p (a b)"), op=ALU.mult), [ta])
                        return (qb, b0, bt, nkb, pi_, PT, td)

                    def at_PV(stt):
                        qb, b0, bt, nkb, pi_, PT, td = stt
                        nb = len(bt)
                        t_ = None
                        for i, kb in enumerate(bt):
                            last = (g == 2 and b0 + 4 >= nkb and i == nb - 1)
                            vap = VB[:, kb, :] if (pas == 0 or kb >= 8) else prev[2][:, kb - prev[0], :]
                            mm_group(NUM(qb), [(vap, PT[:, i * 128:(i + 1) * 128])], deps=[td], sig=False, flags=(False, last))
                            t_ = mm_group(DEN(qb), [(ones[:, :], PT[:, i * 128:(i + 1) * 128])], sig=(i == nb - 1), flags=(False, last))
                        ptrot.done(pi_, [t_])
                        return t_

                    tpv = None
                    pendq = []
                    for qb in range(8):
                        qbg = pas * 8 + qb
                        kbs = [qbg - dl for dl in range(NDEL[g]) if qbg - dl >= 0]
                        for b0 in range(0, len(kbs), 4):
                            pendq.append(at_S(qb, b0, kbs[b0:b0 + 4], len(kbs)))
                            if len(pendq) > 2:
                                tpv = at_PV(pendq.pop(0))
                    while pendq:
                        tpv = at_PV(pendq.pop(0))
                    if tpv is None:
                        tpv = em.last["pe"]
                    att_reads.append(tpv)
                    kt_free = att_reads
                    if pas == 1 and "noprev" not in DBG:
                        ring.release(kpli, tpv)
                        ring.release(vpli, tpv)
                fin_tok = []
                for hf in range(2 if "noattn" not in DBG else 0):
                    td = dve(lambda e, hf=hf: e.reciprocal(out=t2b[:, :], in_=PS[:, 6 + hf, :]), [tpv])
                    td = dve(lambda e, hf=hf, h=h: e.tensor_tensor(out=OT[:, h, hf * 512:(hf + 1) * 512], in0=PS[:, 4 + hf, :], in1=t2b[:, :], op=ALU.mult), [td])
                    fin_tok = [td]
                if pas == 0 and "nosamp" not in DBG:
                    NUMS = PS[:, 4, 0:NS]
                    DENS = PS[:, 4, NS:2 * NS]
                    tlast = None
                    mm_group(PS[:, 4, 0:2 * NS], [(zer[:, :], hT[:, 0, M:M + 2 * NS])], deps=fin_tok)
                    for g in range(3):
                        L = WINS[g]
                        nblk = L // 128
                        kcs, kct, kli = ring.get(kc_in[g][ja, h], lambda s, L=L: s[:, 0:L])
                        vcs, vct, vli = ring.get(vc_in[g][ja, h], lambda s, L=L, nblk=nblk: s[:, 0:L].rearrange("p (n e) -> p n e", n=nblk))
                        xi, pS, xlast = auxrot.next()
                        tS = None
                        for kb in range(nblk):
                            tS = mm_group(pS[:, kb * NS:(kb + 1) * NS], [(kcs[:, kb * 128:(kb + 1) * 128], smallb[:, 0, g, :])],
                                          deps=[kct, vct] + xlast + fin_tok if kb == 0 else (), sig=(kb == nblk - 1))
                        pi_, PT, ptlast = ptrot.next()
                        ta = act(lambda e, PT=PT, pS=pS, nblk=nblk: e.activation(out=PT[:, 0:nblk * NS], in_=pS[:, 0:nblk * NS], func=AF.Exp, scale=SC),
                                 [tS] + ptlast)
                        auxrot.done(xi, [ta])
                        td = dve(lambda e, PT=PT, nblk=nblk, g=g: e.tensor_tensor(out=PT[:, 0:nblk * NS], in0=PT[:, 0:nblk * NS],
                                                                                 in1=smaskb[:, g, 0:nblk * NS], op=ALU.mult), [ta])
                        for kb in range(nblk):
                            first = False
                            mm_group(NUMS, [(vcs[:, kb, :], PT[:, kb * NS:(kb + 1) * NS])], deps=[td], sig=False, flags=(first, False))
                            tlast = mm_group(DENS, [(ones[:, :], PT[:, kb * NS:(kb + 1) * NS])], sig=(kb == nblk - 1), flags=(first, False))
                        ptrot.done(pi_, [tlast])
                        ring.release(kli, tlast)
                        ring.release(vli, tlast)
                        xi, pS, xlast = auxrot.next()
                        tS = mm_group(pS[0:NS, 0:NS], [(smallb[:, 1, g, :], smallb[:, 0, g, :])], deps=xlast)
                        pi_, PT, ptlast = ptrot.next()
                        ta = act(lambda e, PT=PT, pS=pS: e.activation(out=PT[0:NS, 0:NS], in_=pS[0:NS, 0:NS], func=AF.Exp, scale=SC), [tS] + ptlast)
                        auxrot.done(xi, [ta])
                        td = dve(lambda e, PT=PT, g=g: e.tensor_tensor(out=PT[0:NS, 0:NS], in0=PT[0:NS, 0:NS], in1=nmaskb[0:NS, g, :], op=ALU.mult), [ta])
                        lastg = (g == 2)
                        mm_group(NUMS, [(vsb[0:NS, g, :], PT[0:NS, 0:NS])], deps=[td], sig=False, flags=(False, lastg))
                        tlast = mm_group(DENS, [(ones[0:NS, :], PT[0:NS, 0:NS])], flags=(False, lastg))
                        ptrot.done(pi_, [tlast])
                    td = dve(lambda e: e.reciprocal(out=sm32[:, 24:28], in_=DENS), [tlast])
                    td = dve(lambda e, h=h: e.tensor_tensor(out=OT[:, h, TB:TB + NS], in0=NUMS, in1=sm32[:, 24:28], op=ALU.mult), [td])
                    fin_tok = [td]
                    samp_last = [tlast]
            em.barrier()
            accrot = Rot([bank(4), bank(5)])
            for mo in range(KC):
                slot, wtok, li = ring.get(awo[ja, mo], lambda s: s[:, 0:H * 128].rearrange("p (k c) -> p k c", k=H))
                tl = None
                for (c0, w) in tiles(pas):
                    ci, pa, clast = accrot.next()
                    tl = mm_group(pa[:, :w], [(slot[:, hh, :], OT[:, hh, c0:c0 + w]) for hh in range(H)], deps=[wtok] + clast)
                    tx = dve(lambda e, pa=pa, mo=mo, c0=c0, w=w: e.tensor_tensor(out=xT[:, mo, c0:c0 + w], in0=pa[:, :w], in1=xT[:, mo, c0:c0 + w],
                                                                                op=ALU.add), [tl])
                    accrot.done(ci, [tx])
                ring.release(li, tl)
            em.barrier()

        scr_w = {(g, h): [] for g in range(3) for h in range(H)}
        stage = [0]

        def stop():
            stage[0] += 1
            return upto is not None and stage[0] > upto

        for pas in range(2):
            xv = x_in.rearrange("(k p) t -> p k t", p=128)
            for a in range(0, KC, 4):
                em.dma(xT[:, a:a + 4, 0:TB], xv[:, a:a + 4, pas * TB:(pas + 1) * TB])
            if pas == 0:
                em.dma(xT[:, :, TB:TB + NS], xs_in.rearrange("(k p) s -> p k s", p=128))
            em.barrier()
            done = False
            for l in range(cfg.DEPTH):
                kind, j = l % 3, l // 3
                if stop():
                    done = True
                    break
                rmsnorm(pas, l)
                ffn(pas, w1i[l], w1o[l])
                if stop():
                    done = True
                    break
                rmsnorm(pas, cfg.DEPTH + l)
                if kind == 0:
                    pool_mixer(pas, j, cfg.DEPTH + l, 3 * cfg.DEPTH + j)
                elif kind == 1:
                    chunk_mixer(pas, j, 3 * cfg.DEPTH + NPL + j)
                else:
                    attn_mixer(pas, j)
                if stop():
                    done = True
                    break
                rmsnorm(pas, 2 * cfg.DEPTH + l)
                ffn(pas, w2i[l], w2o[l])
            stage[0] = 0
            yv = y_out.rearrange("(k p) t -> p k t", p=128)
            for a in range(0, KC, 4):
                em.dma(yv[:, a:a + 4, pas * TB:(pas + 1) * TB], xT[:, a:a + 4, 0:TB])
            if pas == 0:
                em.dma(ys_out.rearrange("(k p) s -> p k s", p=128), xT[:, :, TB:TB + NS])
            em.barrier()
        ring.finalize()
        final = em.all_dma_toks()

        def semof(key):
            kind, i = key
            if kind == "e":
                return esem[i]
            if kind == "d":
                return dsem[i]
            return wsem[i]

        def run(eng, name):
            for fn, waits, sig in em.ops[name]:
                for key, val in waits:
                    eng.wait_ge(semof(key), val)
                if fn is None:
                    continue
                r = fn(eng)
                if isinstance(r, tuple) and r[0] == "dma":
                    _, o, s_, k = r
                    eng.dma_start(out=o, in_=s_).then_inc(semof(k), 16)
                elif sig:
                    r.then_inc(esem[name], 1)
            if name == "sp":
                for key, val in final:
                    eng.wait_ge(semof(key), val)

        @block.tensor
        def _(t):
            run(t, "pe")

        @block.scalar
        def _(s):
            run(s, "act")

        @block.vector
        def _(v):
            run(v, "dve")

        @block.gpsimd
        def _(g):
            run(g, "pool")

        @block.sync
        def _(sp):
            run(sp, "sp")
    return nc


def coltile(W):
    K, N = W.shape
    return np.ascontiguousarray(W.reshape(K // 128, 128, N // 128, 128).transpose(2, 1, 0, 3))


def make_consts():
    c = np.zeros((128, 320), np.float32)
    cm = np.zeros((128, 1740), np.float32)
    k = np.arange(128)[:, None]
    q = np.arange(128)[None, :]
    c[:, 0:128] = (k == (q + 64) % 128)
    for gi, w in enumerate((2, 4, 8, 16)):
        c[:, 128 + gi * 16:128 + gi * 16 + 16] = 1.0 / np.minimum(w, np.arange(16) + 1)
    c[:, 192:320] = (k <= q)
    ms = []
    ms.append(q >= k)
    ms.append(q <= k)
    m4 = ((q - k) % 4 == 0)
    ms += [m4 & (q >= k), m4, m4, m4, m4 & (q <= k)]
    m16 = ((q - k) % 16 == 0)
    ms += [m16 & (q >= k), m16, m16, m16, m16]
    cm[:, 0:1536] = np.concatenate([m.astype(np.float32) for m in ms], axis=1)
    i = np.arange(128)[:, None]
    t = np.arange(4)[None, :]
    sm = [np.tile((i >= t), (1, 16)), np.tile((i % 4 == t), (1, 16)), np.tile((i % 16 == t), (1, 16))]
    cm[:, 1536:1536 + 192] = np.concatenate([m.astype(np.float32) for m in sm], axis=1)
    kk = np.arange(4)[:, None]
    nm = [(kk <= t), (kk == t), (kk == t)]
    cm[0:4, 1728:1740] = np.concatenate([m.astype(np.float32) for m in nm], axis=1)
    return c, cm


def make_rope():
    half = 64
    freqs = (np.float32(10000.0) ** (-2.0 * np.arange(half, dtype=np.float32) / np.float32(128))).astype(np.float32)
    pos = np.concatenate([np.arange(SEQ), PAST + np.arange(NS)]).astype(np.float32)
    ang = (pos[None, :] * freqs[:, None]).astype(np.float32)
    cos = np.cos(ang).astype(np.float32)
    sin = np.sin(ang).astype(np.float32)
    r = np.zeros((128, 2, SEQ + NS), np.float32)
    r[:64, 0] = cos
    r[64:, 0] = cos
    r[:64, 1] = -sin
    r[64:, 1] = sin
    return r


def prep_inputs(cfg, inp):
    D, KC, NJ, H = cfg.D, cfg.KC, cfg.NJ, cfg.H
    f = lambda a: np.ascontiguousarray(np.asarray(a, dtype=np.float32))
    sh = {}
    vl = [inp["norm_ffn1"], inp["norm_mix"], inp["norm_ffn2"], inp["pool_scale"], inp["chunk_v_norm"]]
    allv = np.concatenate([f(v) for v in vl], axis=0)
    sh["vecs"] = np.ascontiguousarray(allv.reshape(cfg.NV, KC, 128).transpose(2, 0, 1))
    sh["consts"], sh["cmask"] = make_consts()
    sh["rope"] = make_rope()

    def win(w):
        w = f(w)
        out = np.empty((cfg.DEPTH, NJ, 128, 2, KC, 128), np.float32)
        for l in range(cfg.DEPTH):
            ct = coltile(w[l])
            out[l, :, :, 0] = ct[:NJ]
            out[l, :, :, 1] = ct[NJ:]
        return out.reshape(cfg.DEPTH, NJ, 128, 2 * KC * 128)

    sh["w1i"] = win(inp["ffn1_w_in"])
    sh["w2i"] = win(inp["ffn2_w_in"])
    sh["w1o"] = f(inp["ffn1_w_out"]).reshape(cfg.DEPTH, NJ, 128, D)
    sh["w2o"] = f(inp["ffn2_w_out"]).reshape(cfg.DEPTH, NJ, 128, D)
    sh["pool_w"] = f(inp["pool_w"])
    cw = f(inp["chunk_w_in"])
    sh["cw_v"] = np.ascontiguousarray(cw[:, :, D:])
    sh["cw_u"] = np.stack([coltile(cw[j, :, :D]).reshape(KC, 128, KC * 128) for j in range(cfg.NCL)])
    sh["cw_o"] = np.stack([coltile(f(inp["chunk_w_out"])[j]).reshape(KC, 128, KC * 128) for j in range(cfg.NCL)])
    sh["c_vg"] = f(inp["chunk_v_norm"])
    sh["c_ws"] = np.ascontiguousarray(f(inp["chunk_w_s"]).transpose(0, 3, 1, 2)).reshape(cfg.NCL, 128, 8 * 128)
    sh["c_bs"] = f(inp["chunk_b_s"]).reshape(cfg.NCL, 8 * 128)
    sh["a_qkv"] = np.stack([coltile(f(inp["attn_w_qkv"])[j]).reshape(9 * H, 128, KC * 128) for j in range(cfg.NAL)])
    sh["a_wo"] = np.stack([coltile(f(inp["attn_w_out"])[j]).reshape(KC, 128, H * 128) for j in range(cfg.NAL)])
    sh["a_gn"] = np.ascontiguousarray(np.stack([f(inp["attn_q_norm"]), f(inp["attn_k_norm"])], axis=-1))
    xp = f(inp["x_prompt"])
    xs = f(inp["x_sample"])
    st = f(inp["state_pool"])
    caches = [f(inp["cache_kv_g0"]), f(inp["cache_kv_g1"]), f(inp["cache_kv_g2"])]
    maps = []
    zx = np.zeros((D, SEQ), np.float32)
    for c in range(cfg.NCORES):
        m = dict(sh)
        m["x_in"] = np.ascontiguousarray(xp[PCORE.index(c)].T) if c in PCORE else zx
        m["xs_in"] = np.ascontiguousarray(xs[c].T)
        m["pool_state"] = np.ascontiguousarray(st[:, c].transpose(0, 2, 1))
        for g in range(3):
            ck = caches[g][:, c]
            m["kcache%d" % g] = np.ascontiguousarray(ck[:, :, 0].transpose(0, 2, 3, 1))
            L = ck.shape[1]
            v = ck[:, :, 1].reshape(cfg.NAL, L // 128, 128, H, 128).transpose(0, 3, 2, 1, 4)
            m["vcache%d" % g] = np.ascontiguousarray(v).reshape(cfg.NAL, H, 128, L)
        maps.append(m)
    return maps


def assemble(cfg, res, nb=4):
    H = cfg.H
    R = lambda c, k: np.asarray(res[c][k], dtype=np.float32)
    DB = cfg.NCORES
    PC = PCORE
    y_p = np.stack([R(PC[b], "y").T for b in range(nb)])
    y_s = np.stack([R(c, "ys").T for c in range(DB)])
    pool_p = np.stack([R(PC[b], "pool_p").transpose(0, 2, 1) for b in range(nb)], axis=1)
    pool_s = np.stack([R(c, "pool_s").transpose(0, 2, 1) for c in range(DB)], axis=1)
    chunk_v = np.stack([R(c, "chunk_v") for c in range(DB)], axis=1)
    outs = [y_p, y_s, pool_p, pool_s, chunk_v]
    for g in range(3):
        kp = np.stack([R(PC[b], "ko%d" % g).transpose(0, 3, 1, 2) for b in range(nb)], axis=1)
        vp = np.stack([R(PC[b], "vo%d" % g) for b in range(nb)], axis=1)
        outs.append(np.ascontiguousarray(np.stack([kp, vp], axis=3)))
        ksm = np.stack([R(c, "kos%d" % g).transpose(0, 3, 1, 2) for c in range(DB)], axis=1)
        vsm = np.stack([R(c, "vos%d" % g) for c in range(DB)], axis=1)
        outs.append(np.ascontiguousarray(np.stack([ksm, vsm], axis=3)))
    return tuple(outs)


def kernel(**inputs):
    cfg = Cfg()
    maps = prep_inputs(cfg, inputs)
    nc = build(cfg)
    res = run_bass_kernel_spmd(nc, maps, core_ids=list(range(cfg.NCORES)))
    return assemble(cfg, res.results)
```

```python
import numpy as np
import concourse.bass as bass
import concourse.mybir as mybir
from concourse.bass_utils import run_bass_kernel_spmd

F32 = mybir.dt.float32
BF16 = mybir.dt.bfloat16
AF = mybir.ActivationFunctionType
ALU = mybir.AluOpType

EPS = 1e-6
M = 16
TB = 1024
NS = 4
SEQ = 2048
PAST = 16384
NSLOT = 4
SLOTW = 4096
ND = 24
WINS = (128, 512, 2048)
DILS = (1, 4, 16)
NDEL = (2, 5, 16)
KEEP = (128, 512, 2048)
MASK_BASE = {0: (0, None), 1: (2, 6), 2: (7, 8)}
DBG = set()
PCORE = (0, 1, 4, 5)


class Cfg:
    def __init__(self, D=2048, DFF=5504, H=16, DEPTH=4, NCORES=8):
        self.D, self.DFF, self.H, self.DEPTH, self.NCORES = D, DFF, H, DEPTH, NCORES
        self.KC = D // 128
        self.NJ = DFF // 128
        self.AW = 3 * H * 128
        self.NPL = (DEPTH + 2) // 3
        self.NCL = (DEPTH + 1) // 3
        self.NAL = DEPTH // 3
        self.NV = 3 * DEPTH + self.NPL + self.NCL


ENG = ("pe", "act", "dve", "pool", "sp")


class Em:
    def __init__(self):
        self.ops = {e: [] for e in ENG}
        self.cnt = {e: 0 for e in ENG}
        self.seen = {e: {} for e in ENG}
        self.last = {e: None for e in ENG}
        self.ndma = 0
        self.dval = [0] * ND

    def op(self, eng, fn, deps=(), sig=True):
        waits = []
        for d in deps:
            if d is None:
                continue
            key, val = d
            if self.seen[eng].get(key, 0) >= val:
                continue
            self.seen[eng][key] = val
            waits.append((key, val))
        tok = None
        if sig:
            self.cnt[eng] += 1
            tok = (("e", eng), self.cnt[eng])
            self.last[eng] = tok
        self.ops[eng].append((fn, waits, sig))
        return tok

    def dma(self, out, in_, deps=(), q="sp"):
        i = self.ndma % ND
        self.ndma += 1
        key = ("d", i)
        prev = self.dval[i]
        self.dval[i] += 16
        deps = list(deps)
        if prev:
            deps.append((key, prev))
        val = self.dval[i]
        self.op(q, lambda e, o=out, s=in_, k=key: ("dma", o, s, k), deps=deps, sig=False)
        return (key, val)

    def all_dma_toks(self):
        return [(("d", i), v) for i, v in enumerate(self.dval) if v]

    def barrier(self):
        toks = [self.last[e] for e in ("pe", "act", "dve")] + self.all_dma_toks()
        for e in ("pe", "act", "dve", "sp"):
            for d in toks:
                if d is None:
                    continue
                key, val = d
                if self.seen[e].get(key, 0) >= val:
                    continue
                self.seen[e][key] = val
                self.ops[e].append((None, [(key, val)], False))


class Rot:
    def __init__(self, aps):
        self.aps = list(aps)
        self.last = [[] for _ in self.aps]
        self.i = 0

    def next(self):
        i = self.i % len(self.aps)
        self.i += 1
        return i, self.aps[i], list(self.last[i])

    def done(self, i, toks):
        self.last[i] = [t for t in toks if t is not None]


class Ring:
    def __init__(self, em, slots):
        self.em, self.slots = em, slots
        self.loads, self.rel = [], []

    def get(self, src, view, deps=()):
        i = len(self.loads)
        slot = i % NSLOT
        dst = view(self.slots[slot])
        self.loads.append((src, dst, slot, list(deps)))
        self.rel.append(None)
        return dst, (("w", slot), 16 * (i // NSLOT + 1)), i

    def release(self, i, tok):
        assert tok is not None
        self.rel[i] = tok

    def finalize(self):
        for i, (src, dst, slot, xd) in enumerate(self.loads):
            assert self.rel[i] is not None, i
            deps = xd + ([self.rel[i - NSLOT]] if i >= NSLOT else [])
            self.em.op("pool", lambda e, o=dst, s=src, k=("w", slot): ("dma", o, s, k), deps=deps, sig=False)


def build(cfg, upto=None):
    D, KC, NJ, H = cfg.D, cfg.KC, cfg.NJ, cfg.H
    NPL, NCL, NAL = cfg.NPL, cfg.NCL, cfg.NAL
    nc = bass.Bass("TRN2", target_bir_lowering=False)

    def din(name, shape):
        return nc.dram_tensor(name, list(shape), F32, kind="ExternalInput").ap()

    def dout(name, shape):
        return nc.dram_tensor(name, list(shape), F32, kind="ExternalOutput").ap()

    x_in = din("x_in", [D, SEQ])
    xs_in = din("xs_in", [D, NS])
    vecs_in = din("vecs", [128, cfg.NV, KC])
    consts_in = din("consts", [128, 320])
    cmask_in = din("cmask", [128, 1740])
    rope_in = din("rope", [128, 2, SEQ + NS])
    w1i = din("w1i", [cfg.DEPTH, NJ, 128, 2 * KC * 128])
    w1o = din("w1o", [cfg.DEPTH, NJ, 128, D])
    w2i = din("w2i", [cfg.DEPTH, NJ, 128, 2 * KC * 128])
    w2o = din("w2o", [cfg.DEPTH, NJ, 128, D])
    pw_in = din("pool_w", [NPL, 4, D // 4, D // 4])
    pstate_in = din("pool_state", [NPL, D, 15])
    cwin_v = din("cw_v", [NCL, D, D])
    cwin_u = din("cw_u", [NCL, KC, 128, KC * 128])
    cwo = din("cw_o", [NCL, KC, 128, KC * 128])
    cvg_in = din("c_vg", [NCL, D])
    cws_in = din("c_ws", [NCL, 128, 8 * 128])
    cbs_in = din("c_bs", [NCL, 8 * 128])
    aqkv = din("a_qkv", [NAL, 9 * H, 128, KC * 128])
    awo = din("a_wo", [NAL, KC, 128, H * 128])
    agn_in = din("a_gn", [NAL, 128, 2])
    kc_in = [din("kcache%d" % g, [NAL, H, 128, WINS[g]]) for g in range(3)]
    vc_in = [din("vcache%d" % g, [NAL, H, 128, WINS[g]]) for g in range(3)]

    y_out = dout("y", [D, SEQ])
    ys_out = dout("ys", [D, NS])
    pp_out = dout("pool_p", [NPL, D, 15])
    ps_out = dout("pool_s", [NPL, D, 15])
    cv_out = dout("chunk_v", [NCL, NS, D])
    ko = [dout("ko%d" % g, [NAL, H, 128, KEEP[g]]) for g in range(3)]
    vo = [dout("vo%d" % g, [NAL, KEEP[g], H, 128]) for g in range(3)]
    kos = [dout("kos%d" % g, [NAL, H, 128, NS]) for g in range(3)]
    vos = [dout("vos%d" % g, [NAL, NS, H, 128]) for g in range(3)]
    kscr = dout("kscr", [3, H, 128, TB])
    vscr = dout("vscr", [3, H, 128, 8 * 128])

    NT0 = TB + NS
    em = Em()
    S32 = 6160
    S16 = 7176
    import contextlib
    with contextlib.ExitStack() as st:
        xT = st.enter_context(nc.sbuf_tensor("xT", [128, KC, NT0], F32))
        hT = st.enter_context(nc.sbuf_tensor("hT", [128, KC, M + NT0], BF16))
        AB = st.enter_context(nc.sbuf_tensor("AB", [128, 16 * NT0], BF16))
        RG = st.enter_context(nc.sbuf_tensor("RG", [128, NSLOT, SLOTW], BF16))
        s32 = st.enter_context(nc.sbuf_tensor("s32", [128, S32], F32))
        s16 = st.enter_context(nc.sbuf_tensor("s16", [128, S16], BF16))
        vsb2 = st.enter_context(nc.sbuf_tensor("vsb2", [128, 3 * 128], BF16))
        vsb = vsb2[:, :].rearrange("p (g e) -> p g e", g=3)
        vecs = st.enter_context(nc.sbuf_tensor("vecs_sb", [128, cfg.NV, KC], F32))
        cst = st.enter_context(nc.sbuf_tensor("cst", [128, 320], F32))
        ones = st.enter_context(nc.sbuf_tensor("ones", [128, 128], BF16))
        zer = st.enter_context(nc.sbuf_tensor("zer", [128, 128], BF16))
        masks = st.enter_context(nc.sbuf_tensor("masks", [128, 12, 128], BF16))
        smaskb = st.enter_context(nc.sbuf_tensor("smaskb", [128, 3, 64], BF16))
        nmaskb = st.enter_context(nc.sbuf_tensor("nmaskb", [128, 3, 4], BF16))
        halo = st.enter_context(nc.sbuf_tensor("halo", [128, max(NPL, 1), KC, 15], BF16))
        agn = st.enter_context(nc.sbuf_tensor("agn", [128, max(NAL, 1), 2], F32))
        smallb = st.enter_context(nc.sbuf_tensor("smallb", [128, 3, 3, 4], BF16))
        sm32 = st.enter_context(nc.sbuf_tensor("sm32", [128, 40], F32))
        PS = st.enter_context(nc.psum_tensor("PS", [128, 8, 512], F32))
        esem = {e: st.enter_context(nc.semaphore("sem_" + e)) for e in ("pe", "act", "dve")}
        dsem = [st.enter_context(nc.semaphore("dsem%d" % i)) for i in range(ND)]
        wsem = [st.enter_context(nc.semaphore("wsem%d" % i)) for i in range(NSLOT)]
        block = st.enter_context(nc.Block())

        ring = Ring(em, [RG[:, i, :] for i in range(NSLOT)])
        C_PSW = cst[:, 0:128]
        C_INVC = cst[:, 128:192]
        C_TRI = cst[:, 192:320]
        C_MASK = s32[:, 0:1536]
        C_SM = s32[:, 1536:1536 + 192]
        C_NM = s32[:, 1728:1728 + 12]

        def V(idx, kc):
            return vecs[:, idx, kc:kc + 1]

        def act(fn, deps=()):
            return em.op("act", fn, deps)

        def dve(fn, deps=()):
            return em.op("dve", fn, deps)

        def mm_group(out, pairs, deps=(), sig=True, flags=None):
            n = len(pairs)
            tok = None
            for i, (l, r) in enumerate(pairs):
                s0 = (i == 0) if flags is None else flags[0] and i == 0
                s1 = (i == n - 1) if flags is None else flags[1] and i == n - 1
                tok = em.op("pe", lambda t, o=out, l=l, r=r, a=s0, b=s1, sk=(flags is not None): t.matmul(o, l, r, start=a, stop=b, skip_group_check=sk),
                            deps=deps if i == 0 else (), sig=(sig and i == n - 1))
            return tok

        def bank(i):
            return PS[:, i, :]

        def tiles(pas):
            if pas == 0:
                return [(0, 343), (343, 343), (686, 342)]
            return [(0, 512), (512, 512)]

        def atiles(pas):
            t = [(0, 512), (512, 512)]
            if pas == 0:
                t.append((TB, NS))
            return t

        em.dma(cst[:, :], consts_in[:, :])
        t_c = em.dma(s32[:, 0:1740], cmask_in[:, :])
        t_v = em.dma(vecs[:, :, :], vecs_in[:, :, :])
        if NAL:
            t_g = em.dma(agn[:, :, :], agn_in.rearrange("l p t -> p l t"))
        dve(lambda e: e.memset(ones[:, :], 1.0))
        dve(lambda e: e.memset(zer[:, :], 0.0))
        dve(lambda e: e.tensor_copy(out=masks[:, :, :], in_=C_MASK.rearrange("p (a b) -> p a b", a=12)), [t_c])
        dve(lambda e: e.tensor_copy(out=smaskb[:, :, :], in_=C_SM.rearrange("p (a b) -> p a b", a=3)), [t_c])
        dve(lambda e: e.tensor_copy(out=nmaskb[:, :, :], in_=C_NM.rearrange("p (a b) -> p a b", a=3)), [t_c])
        dve(lambda e: e.memset(hT[:, :, 0:M], 0.0))
        em.barrier()

        rstd = s32[:, 0:NT0]
        FS = 1032

        def rmsnorm(pas, vidx):
            NT = TB + (NS if pas == 0 else 0)
            ta = None
            tb_ = None
            for kc in range(KC):
                if kc % 2 == 0:
                    ta = act(lambda e, kc=kc: e.activation(out=hT[:, kc, M:M + NT], in_=xT[:, kc, 0:NT], func=AF.Square))
                else:
                    tb_ = dve(lambda e, kc=kc: e.tensor_tensor(out=hT[:, kc, M:M + NT], in0=xT[:, kc, 0:NT], in1=xT[:, kc, 0:NT], op=ALU.mult))
            rot = Rot([bank(6), bank(7)])
            tds = []
            for (c0, w) in tiles(pas):
                i, ps, lst = rot.next()
                tp = mm_group(ps[:, :w], [(ones[:, :], hT[:, kc, M + c0:M + c0 + w]) for kc in range(KC)], deps=[ta, tb_] + lst)
                td = act(lambda e, ps=ps, c0=c0, w=w: e.activation(out=rstd[:, c0:c0 + w], in_=ps[:, :w], func=AF.Sqrt,
                                                                    scale=1.0 / D, bias=EPS), [tp])
                td = dve(lambda e, c0=c0, w=w: e.reciprocal(out=rstd[:, c0:c0 + w], in_=rstd[:, c0:c0 + w]), [td])
                rot.done(i, [td])
                tds.append(td)
                tpl = tp
            for kc in range(KC):
                dve(lambda e, kc=kc: e.scalar_tensor_tensor(out=hT[:, kc, M:M + NT], in0=xT[:, kc, 0:NT], scalar=V(vidx, kc),
                                                            in1=rstd[:, 0:NT], op0=ALU.mult, op1=ALU.mult), [tpl] + tds)
            em.barrier()

        def ffn(pas, wi, wo):
            NT = TB + (NS if pas == 0 else 0)
            GS = 6
            groups = [list(range(a, min(a + GS, NJ))) for a in range(0, NJ, GS)]
            abrot = Rot([AB[:, 0:8 * NT0].rearrange("p (a t) -> p a t", a=8), AB[:, 8 * NT0:16 * NT0].rearrange("p (a t) -> p a t", a=8)])
            mmrot = Rot([(bank(0), bank(1)), (bank(2), bank(3))])
            accrot = Rot([bank(4), bank(5), bank(6), bank(7)])
            sgrot = Rot([s32[:, FS:FS + 512], s32[:, FS + 512:FS + 1024]])
            def win_phase(grp):
                ai, ab, alast = abrot.next()
                tdl = None
                for jj, j in enumerate(grp):
                    slot, wtok, li = ring.get(wi[j], lambda s: s[:, 0:2 * KC * 128].rearrange("p (a k m) -> p a k m", a=2, k=KC))
                    tu = None
                    for (c0, w) in tiles(pas):
                        pi, (pg, pu), plast = mmrot.next()
                        mm_group(pg[:, :w], [(slot[:, 0, kc, :], hT[:, kc, M + c0:M + c0 + w]) for kc in range(KC)],
                                 deps=[wtok] + plast, sig=False)
                        tu = mm_group(pu[:, :w], [(slot[:, 1, kc, :], hT[:, kc, M + c0:M + c0 + w]) for kc in range(KC)])
                        si, sg, slast = sgrot.next()
                        ta = act(lambda e, sg=sg, pg=pg, w=w: e.activation(out=sg[:, :w], in_=pg[:, :w], func=AF.Silu), [tu] + slast)
                        tdl = dve(lambda e, ab=ab, jj=jj, c0=c0, w=w, sg=sg, pu=pu: e.tensor_tensor(
                            out=ab[:, jj, c0:c0 + w], in0=sg[:, :w], in1=pu[:, :w], op=ALU.mult), [ta] + alast)
                        mmrot.done(pi, [tdl])
                        sgrot.done(si, [tdl])
                    ring.release(li, tu)
                return (grp, ai, ab, tdl)

            def wout_phase(stt):
                grp, ai, ab, tdl = stt
                wsl = []
                for a in range(0, len(grp), 2):
                    n = min(2, len(grp) - a)
                    j0 = grp[a]
                    slot, wtok, li = ring.get(wo[j0:j0 + n].rearrange("a p n -> p a n"),
                                              lambda s, n=n: s[:, 0:n * D].rearrange("p (a n) -> p a n", a=n))
                    wsl.append((slot, wtok, li))
                tl = None
                for m in range(KC):
                    for (c0, w) in tiles(pas):
                        ci, pa, clast = accrot.next()
                        tl = mm_group(pa[:, :w], [(wsl[jj // 2][0][:, jj % 2, m * 128:(m + 1) * 128], ab[:, jj, c0:c0 + w])
                                                  for jj in range(len(grp))],
                                      deps=[x[1] for x in wsl] + [tdl] + clast)
                        tx = dve(lambda e, pa=pa, m=m, c0=c0, w=w: e.scalar_tensor_tensor(
                            out=xT[:, m, c0:c0 + w], in0=pa[:, :w], scalar=0.5, in1=xT[:, m, c0:c0 + w],
                            op0=ALU.mult, op1=ALU.add), [tl])
                        accrot.done(ci, [tx])
                for (_, _, li) in wsl:
                    ring.release(li, tl)
                abrot.done(ai, [tl])

            pend = None
            for grp in groups:
                stt = win_phase(grp)
                if pend is not None:
                    wout_phase(pend)
                pend = stt
            wout_phase(pend)
            em.barrier()

        def pool_mixer(pas, jp, vmix, vscale):
            NT = TB + (NS if pas == 0 else 0)
            KG = KC // 4
            PG = D // 4
            C = M + TB
            T1 = s32[:, FS:FS + C]
            T2 = s32[:, FS + C:FS + 2 * C]
            HB = s32[:, FS + 2 * C:FS + 2 * C + KC * 19].rearrange("p (k c) -> p k c", k=KC)
            TS = s32[:, FS + 2 * C + KC * 19:FS + 2 * C + KC * 19 + 64]
            P32 = s32[:, FS + 2 * C + KC * 19 + 64:FS + 2 * C + KC * 19 + 64 + KC * 15].rearrange("p (k c) -> p k c", k=KC)
            pooled = AB[:, 0:KC * NT0].rearrange("p (k t) -> p k t", k=KC)
            outs = []
            if pas == 0:
                dve(lambda e: e.tensor_copy(out=halo[:, jp, :, :], in_=hT[:, :, M + TB - 15:M + TB]))
                t_st = em.dma(HB[:, :, 0:15], pstate_in[jp].rearrange("(k p) r -> p k r", p=128))
                for kc in range(KC):
                    dve(lambda e, kc=kc: e.scalar_tensor_tensor(out=HB[:, kc, 15:19], in0=xT[:, kc, TB:TB + NS], scalar=V(vmix, kc),
                                                                in1=rstd[:, TB:TB + NS], op0=ALU.mult, op1=ALU.mult), [t_st])
                outs.append(em.dma(ps_out[jp].rearrange("(k p) r -> p k r", p=128), HB[:, :, 4:19], deps=[em.last["dve"]]))
            else:
                dve(lambda e: e.tensor_copy(out=hT[:, :, 1:M], in_=halo[:, jp, :, :]))
                for kc in range(KC):
                    dve(lambda e, kc=kc: e.scalar_tensor_tensor(out=P32[:, kc, :], in0=xT[:, kc, TB - 15:TB], scalar=V(vmix, kc),
                                                                in1=rstd[:, TB - 15:TB], op0=ALU.mult, op1=ALU.mult))
                outs.append(em.dma(pp_out[jp].rearrange("(k p) r -> p k r", p=128), P32[:, :, :], deps=[em.last["dve"]]))

            def pool_cols(a, ch, Cn, t0, dst, gi, fix, prev):
                w = 2 ** (gi + 1)
                tk = dve(lambda e: e.tensor_tensor(out=T1[:, ch + 1:Cn], in0=a[:, ch + 1:Cn], in1=a[:, ch:Cn - 1], op=ALU.add), [prev])
                cur, oth, lo, sh = T1, T2, ch + 1, 2
                while sh < w:
                    lo += sh
                    tk = dve(lambda e, cur=cur, oth=oth, lo=lo, sh=sh: e.tensor_tensor(
                        out=oth[:, lo:Cn], in0=cur[:, lo:Cn], in1=cur[:, lo - sh:Cn - sh], op=ALU.add), [tk])
                    cur, oth = oth, cur
                    sh *= 2
                n = Cn - t0
                tk = dve(lambda e, cur=cur: e.scalar_tensor_tensor(out=dst[:, 0:n], in0=cur[:, t0:Cn], scalar=1.0 / w, in1=a[:, t0:Cn],
                                                                   op0=ALU.mult, op1=ALU.subtract), [tk])
                if fix:
                    tk = dve(lambda e, cur=cur: e.tensor_tensor(out=TS[:, 0:15], in0=cur[:, t0:t0 + 15],
                                                                in1=C_INVC[:, gi * 16:gi * 16 + 15], op=ALU.mult), [tk])
                    tk = dve(lambda e: e.tensor_tensor(out=dst[:, 0:15], in0=TS[:, 0:15], in1=a[:, t0:t0 + 15], op=ALU.subtract), [tk])
                return tk

            tk = em.last["dve"]
            for kc in range(KC):
                gi = kc // KG
                tk = pool_cols(hT[:, kc, 0:C], 1, C, M, pooled[:, kc, 0:TB], gi, pas == 0, tk)
                if pas == 0:
                    tk = pool_cols(HB[:, kc, :], 0, 19, 15, pooled[:, kc, TB:TB + NS], gi, False, tk)
            accrot = Rot([bank(4), bank(5)])
            for gi in range(4):
                slot, wtok, li = ring.get(pw_in[jp, gi].rearrange("(a p) d -> p a d", p=128),
                                          lambda s: s[:, 0:KG * PG].rearrange("p (a d) -> p a d", a=KG))
                tl = None
                for mo in range(KG):
                    for (c0, w) in tiles(pas):
                        ci, pa, clast = accrot.next()
                        tl = mm_group(pa[:, :w], [(slot[:, a, mo * 128:(mo + 1) * 128], pooled[:, gi * KG + a, c0:c0 + w]) for a in range(KG)],
                                      deps=[wtok, tk] + clast)
                        tx = dve(lambda e, pa=pa, m=gi * KG + mo, c0=c0, w=w: e.scalar_tensor_tensor(
                            out=xT[:, m, c0:c0 + w], in0=pa[:, :w], scalar=V(vscale, m), in1=xT[:, m, c0:c0 + w],
                            op0=ALU.mult, op1=ALU.add), [tl])
                        accrot.done(ci, [tx])
                ring.release(li, tl)
            em.barrier()

        def chunk_mixer(pas, jc, vgidx):
            G8 = KC // 8
            vg_bc = s32[:, 0:D]
            b_bc = s32[:, D:D + 1024].rearrange("p (g q) -> p g q", g=8)
            tmp = s32[:, D + 1024:D + 2048]
            vs32 = s32[:, D + 2048:2 * D + 2048]
            wsT = s16[:, 0:1024].rearrange("p (g q) -> p g q", g=8)
            vs16 = s16[:, 1024:1024 + D]
            junk = s16[:, 1024 + D:1024 + 2 * D]
            Us = s16[:, 1024 + 2 * D:1024 + 2 * D + KC * NS].rearrange("p (k s) -> p k s", k=KC)
            ss = sm32[:, 0:8]
            rs = sm32[:, 8:16]
            Vr = AB[:, 0:4 * D].rearrange("p (n f) -> p n f", n=4)
            U = AB[:, 8 * NT0:8 * NT0 + KC * 512].rearrange("p (k t) -> p k t", k=KC)
            t1 = em.dma(vg_bc, cvg_in[jc:jc + 1, :].to_broadcast([128, D]))
            t2 = em.dma(s32[:, D:D + 1024], cbs_in[jc:jc + 1, :].to_broadcast([128, 1024]))
            t3 = em.dma(tmp, cws_in[jc])
            tw = dve(lambda e: e.tensor_tensor(out=wsT, in0=tmp.rearrange("p (g q) -> p g q", g=8),
                                               in1=C_TRI.unsqueeze(1).to_broadcast([128, 8, 128]), op=ALU.mult), [t1, t2, t3])
            mmrot = Rot([bank(0), bank(1), bank(2), bank(3)])
            accrot = Rot([bank(4), bank(5)])
            auxrot = Rot([bank(6), bank(7)])
            tmprot = Rot([tmp[:, 0:512], tmp[:, 512:1024]])
            segs = [(0, 512), (512, 512)]
            for si, (s0, sw) in enumerate(segs):
                samp = (pas == 0 and si == 0)
                tg = None
                for ft in range(D // 512):
                    sl = []
                    for kh in range(KC // 8):
                        slot, wtok, li = ring.get(cwin_v[jc].rearrange("(k p) n -> p k n", p=128)[:, kh * 8:(kh + 1) * 8, ft * 512:(ft + 1) * 512],
                                                  lambda s: s.rearrange("p (k n) -> p k n", k=8))
                        sl.append((slot, wtok, li))
                    tp = None
                    for n in range(4):
                        pi, ps, plast = mmrot.next()
                        tp = mm_group(ps[:, :], [(hT[:, kc, M + s0 + n * 128:M + s0 + (n + 1) * 128], sl[kc // 8][0][:, kc % 8, :]) for kc in range(KC)],
                                      deps=[x[1] for x in sl] + plast + [tw])
                        tg = act(lambda e, ps=ps, n=n, ft=ft: e.activation(out=Vr[:, n, ft * 512:(ft + 1) * 512], in_=ps[:, :], func=AF.Gelu), [tp])
                        mmrot.done(pi, [tg])
                    if samp:
                        pi, ps, plast = mmrot.next()
                        tp = mm_group(ps[0:NS, :], [(hT[:, kc, M + TB:M + TB + NS], sl[kc // 8][0][:, kc % 8, :]) for kc in range(KC)],
                                      deps=plast)
                        tgs = act(lambda e, ps=ps, ft=ft: e.activation(out=vs32[0:NS, ft * 512:(ft + 1) * 512], in_=ps[0:NS, :], func=AF.Gelu), [tp])
                        mmrot.done(pi, [tgs])
                    for (_, _, li) in sl:
                        ring.release(li, tp)
                tn = None
                for n in range(4):
                    ta = act(lambda e, n=n: e.activation(out=junk, in_=Vr[:, n, :], func=AF.Square, accum_out=ss[:, n:n + 1]), [tn])
                    td = act(lambda e, n=n: e.activation(out=rs[:, n:n + 1], in_=ss[:, n:n + 1], func=AF.Sqrt, scale=1.0 / D, bias=EPS), [ta])
                    td = dve(lambda e, n=n: e.reciprocal(out=rs[:, n:n + 1], in_=rs[:, n:n + 1]), [td])
                    tn = dve(lambda e, n=n: e.scalar_tensor_tensor(out=Vr[:, n, :], in0=Vr[:, n, :], scalar=rs[:, n:n + 1], in1=vg_bc,
                                                                   op0=ALU.mult, op1=ALU.mult), [td])
                if samp:
                    ta = act(lambda e: e.activation(out=junk[0:NS, :], in_=vs32[0:NS, :], func=AF.Square, accum_out=ss[0:NS, 4:5]), [tn])
                    td = act(lambda e: e.activation(out=rs[0:NS, 4:5], in_=ss[0:NS, 4:5], func=AF.Sqrt, scale=1.0 / D, bias=EPS), [ta])
                    td = dve(lambda e: e.reciprocal(out=rs[0:NS, 4:5], in_=rs[0:NS, 4:5]), [td])
                    td = dve(lambda e: e.scalar_tensor_tensor(out=vs32[0:NS, :], in0=vs32[0:NS, :], scalar=rs[0:NS, 4:5], in1=vg_bc[0:NS, :],
                                                              op0=ALU.mult, op1=ALU.mult), [td])
                    tn = dve(lambda e: e.tensor_copy(out=vs16[0:NS, :], in_=vs32[0:NS, :]), [td])
                    em.dma(cv_out[jc], vs32[0:NS, :], deps=[td])
                tgl = None
                for m in range(KC):
                    g = m // G8
                    slot, wtok, li = ring.get(cwin_u[jc, m], lambda s: s[:, 0:KC * 128].rearrange("p (k c) -> p k c", k=KC))
                    pi, ps, plast = mmrot.next()
                    tp = mm_group(ps[:, :], [(slot[:, kc, :], hT[:, kc, M + s0:M + s0 + 512]) for kc in range(KC)], deps=[wtok] + plast)
                    tu = act(lambda e, ps=ps, m=m: e.activation(out=U[:, m, :], in_=ps[:, :], func=AF.Gelu), [tp, tgl])
                    mmrot.done(pi, [tu])
                    if samp:
                        pi, ps2, plast = mmrot.next()
                        tp = mm_group(ps2[:, 0:NS], [(slot[:, kc, :], hT[:, kc, M + TB:M + TB + NS]) for kc in range(KC)], deps=plast)
                        tus = act(lambda e, ps2=ps2, m=m: e.activation(out=Us[:, m, :], in_=ps2[:, 0:NS], func=AF.Gelu), [tp])
                        mmrot.done(pi, [tus])
                    ring.release(li, tp)
                    xi, px, xlast = auxrot.next()
                    tmx = None
                    for n in range(4):
                        tmx = mm_group(px[:, n * 128:(n + 1) * 128], [(Vr[:, n, m * 128:(m + 1) * 128], wsT[:, g, :])],
                                       deps=[tn] + xlast if n == 0 else (), sig=(n == 3))
                    ti, tt, tlast = tmprot.next()
                    td = None
                    for n in range(4):
                        td = dve(lambda e, px=px, tt=tt, n=n, g=g: e.tensor_tensor(out=tt[:, n * 128:(n + 1) * 128], in0=px[:, n * 128:(n + 1) * 128],
                                                                                 in1=b_bc[:, g, :], op=ALU.add), [tmx] + tlast)
                    tgl = dve(lambda e, m=m, tt=tt: e.tensor_tensor(out=U[:, m, :], in0=U[:, m, :], in1=tt, op=ALU.mult), [td, tu])
                    auxrot.done(xi, [td])
                    tmprot.done(ti, [tgl])
                    if samp:
                        xi, px, xlast = auxrot.next()
                        tmx = mm_group(px[:, 0:NS], [(vs16[0:NS, m * 128:(m + 1) * 128], wsT[0:NS, g, 0:NS])], deps=[tn] + xlast)
                        td = dve(lambda e, px=px, g=g: e.tensor_tensor(out=sm32[:, 16:20], in0=px[:, 0:NS], in1=b_bc[:, g, 0:NS], op=ALU.add), [tmx])
                        tgl = dve(lambda e, m=m: e.tensor_tensor(out=Us[:, m, :], in0=Us[:, m, :], in1=sm32[:, 16:20], op=ALU.mult), [td, tus])
                        auxrot.done(xi, [td])
                for mo in range(KC):
                    slot, wtok, li = ring.get(cwo[jc, mo], lambda s: s[:, 0:KC * 128].rearrange("p (k c) -> p k c", k=KC))
                    ci, pa, clast = accrot.next()
                    tl = mm_group(pa[:, :], [(slot[:, m, :], U[:, m, :]) for m in range(KC)], deps=[wtok, tgl] + clast)
                    tx = dve(lambda e, pa=pa, mo=mo, s0=s0: e.tensor_tensor(out=xT[:, mo, s0:s0 + 512], in0=pa[:, :], in1=xT[:, mo, s0:s0 + 512],
                                                                            op=ALU.add), [tl])
                    accrot.done(ci, [tx])
                    if samp:
                        ci, pa, clast = accrot.next()
                        tl = mm_group(pa[:, 0:NS], [(slot[:, m, :], Us[:, m, :]) for m in range(KC)], deps=clast)
                        tx = dve(lambda e, pa=pa, mo=mo: e.tensor_tensor(out=xT[:, mo, TB:TB + NS], in0=pa[:, 0:NS], in1=xT[:, mo, TB:TB + NS],
                                                                         op=ALU.add), [tl])
                        accrot.done(ci, [tx])
                    ring.release(li, tl)
                em.barrier()

        def attn_mixer(pas, ja):
            NT = TB + (NS if pas == 0 else 0)
            SC = float(128 ** -0.5)
            o = 0
            cosT = s32[:, o:o + NT0]; o += NT0
            sinT = s32[:, o:o + NT0]; o += NT0
            qgrot = Rot([s32[:, o:o + 512], s32[:, o + 512:o + 1024]]); o += 1024
            rsrot = Rot([s32[:, o:o + 512], s32[:, o + 512:o + 1024]]); o += 1024
            t1b = s32[:, o:o + 512]; o += 512
            t2b = s32[:, o:o + 512]; o += 512
            kstrot = Rot([s32[:, o:o + 512]]); o += 512
            vstrot = Rot([s32[:, o + i * 128:o + (i + 1) * 128] for i in range(4)]); o += 512
            assert o <= S32, o
            b = 0
            qT = s16[:, b:b + NT0]; b += NT0
            KT = s16[:, b:b + SEQ + NS]; b += SEQ + NS
            VB = s16[:, b:b + 2048].rearrange("p (n e) -> p n e", n=16); b += 2048
            brot = Rot([s16[:, b + i * 512:b + (i + 1) * 512] for i in range(4)]); b += 2048
            sqrot = brot
            ptrot = brot
            assert b <= S16, b
            OT = AB[:, 0:H * NT0].rearrange("p (h t) -> p h t", h=H)
            mmrot = Rot([bank(0), bank(1), bank(2), bank(3)])
            auxrot = mmrot
            NUM = lambda qb: PS[:, 4 + qb // 4, (qb % 4) * 128:(qb % 4 + 1) * 128]
            DEN = lambda qb: PS[:, 6 + qb // 4, (qb % 4) * 128:(qb % 4 + 1) * 128]
            tr = [em.dma(cosT[:, 0:TB], rope_in[:, 0, pas * TB:(pas + 1) * TB]), em.dma(sinT[:, 0:TB], rope_in[:, 1, pas * TB:(pas + 1) * TB])]
            if pas == 0:
                tr.append(em.dma(cosT[:, TB:TB + NS], rope_in[:, 0, SEQ:SEQ + NS]))
                tr.append(em.dma(sinT[:, TB:TB + NS], rope_in[:, 1, SEQ:SEQ + NS]))
            samp_last = []
            def odma(out, in_, deps=()):
                if "nokvout" in DBG:
                    return None
                return em.dma(out, in_, deps=deps)
            fin_tok = []
            kt_free = []
            outd = []
            for h in range(H):
                for bk in (4, 5, 6, 7):
                    mm_group(PS[:, bk, :], [(zer[:, :], hT[:, 0, M:M + 512])], deps=fin_tok if bk == 4 else (), sig=(bk == 7))
                for g in range(3):
                    ks = SEQ - KEEP[g]
                    att_reads = []
                    wr_toks = []
                    def qk_A(which, c0, w, slot, wtok):
                        pi, ps, plast = mmrot.next()
                        tq = mm_group(ps[:, :w], [(slot[:, kc, :], hT[:, kc, M + c0:M + c0 + w]) for kc in range(KC)], deps=[wtok] + plast)
                        si, sq, slast = sqrot.next()
                        ta1 = act(lambda e: e.activation(out=sq[:, :w], in_=ps[:, :w], func=AF.Square), [tq] + slast)
                        qi, qg, qlast = qgrot.next()
                        ta2 = act(lambda e: e.activation(out=qg[:, :w], in_=ps[:, :w], func=AF.Identity,
                                                         scale=agn[:, ja, which:which + 1]), qlast)
                        mmrot.done(pi, [ta2])
                        return (which, c0, w, si, sq, ta1, qi, qg, ta2), tq

                    def qk_B(stt):
                        which, c0, w, si, sq, ta1, qi, qg, ta2 = stt
                        xi, pss, xlast = auxrot.next()
                        tp1 = mm_group(pss[:, :w], [(ones[:, :], sq[:, :w])], deps=[ta1] + xlast)
                        sqrot.done(si, [tp1])
                        xj, psw, xlast2 = auxrot.next()
                        tp2 = mm_group(psw[:, :w], [(C_PSW, qg[:, :w])], deps=[ta2] + xlast2)
                        ri, rs_, rlast = rsrot.next()
                        ta3 = act(lambda e: e.activation(out=rs_[:, :w], in_=pss[:, :w], func=AF.Ln, scale=1.0 / 128, bias=EPS), [tp1] + rlast)
                        ta4 = act(lambda e: e.activation(out=rs_[:, :w], in_=rs_[:, :w], func=AF.Exp, scale=-0.5), [ta3])
                        auxrot.done(xi, [ta3])
                        td1 = dve(lambda e: e.tensor_tensor(out=t1b[:, :w], in0=qg[:, :w], in1=cosT[:, c0:c0 + w], op=ALU.mult), [ta2] + tr)
                        td2 = dve(lambda e: e.tensor_tensor(out=t2b[:, :w], in0=psw[:, :w], in1=sinT[:, c0:c0 + w], op=ALU.mult), [tp2])
                        auxrot.done(xj, [td2])
                        td3 = dve(lambda e: e.tensor_tensor(out=t1b[:, :w], in0=t1b[:, :w], in1=t2b[:, :w], op=ALU.add), [td2])
                        qgrot.done(qi, [td3, tp2])
                        if which == 0:
                            dst = qT[:, c0:c0 + w] if c0 < TB else smallb[:, 0, g, :]
                            td4 = dve(lambda e: e.tensor_tensor(out=dst, in0=t1b[:, :w], in1=rs_[:, :w], op=ALU.mult), [ta4] + kt_free + samp_last)
                            rsrot.done(ri, [td4])
                            wr_toks.append(td4)
                        else:
                            ki, kst, klast = kstrot.next()
                            td4 = dve(lambda e: e.tensor_tensor(out=kst[:, :w], in0=t1b[:, :w], in1=rs_[:, :w], op=ALU.mult), [ta4] + klast)
                            rsrot.done(ri, [td4])
                            dst = KT[:, pas * TB + c0:pas * TB + c0 + w] if c0 < TB else smallb[:, 1, g, :]
                            td5 = dve(lambda e: e.tensor_copy(out=dst, in_=kst[:, :w]), [td4] + kt_free + samp_last)
                            wr_toks.append(td5)
                            rd = [td5]
                            if c0 >= TB:
                                rd.append(odma(kos[g][ja, h], kst[:, 0:NS], deps=[td4]))
                            else:
                                tg0 = pas * TB + c0
                                lo = max(tg0, ks)
                                if pas == 0:
                                    t_ = odma(kscr[g, h][:, c0:c0 + w], kst[:, 0:w], deps=[td4])
                                    rd.append(t_)
                                    if t_ is not None:
                                        scr_w[(g, h)].append(t_)
                                if lo < tg0 + w:
                                    rd.append(odma(ko[g][ja, h][:, lo - ks:tg0 + w - ks], kst[:, lo - tg0:w], deps=[td4]))
                            kstrot.done(ki, rd)

                    pend = None
                    for which in (0, 1):
                        slot, wtok, li = ring.get(aqkv[ja, which * 3 * H + g * H + h], lambda s: s[:, 0:KC * 128].rearrange("p (k c) -> p k c", k=KC))
                        tq = None
                        for (c0, w) in atiles(pas):
                            stt, tq = qk_A(which, c0, w, slot, wtok)
                            if pend is not None:
                                qk_B(pend)
                            pend = stt
                        ring.release(li, tq)
                    qk_B(pend)
                    slot, wtok, li = ring.get(aqkv[ja, 2 * 3 * H + g * H + h], lambda s: s[:, 0:KC * 128].rearrange("p (k c) -> p k c", k=KC))
                    tv = None
                    for tb in range(8 if "nov" not in DBG else 0):
                        pi, ps, plast = mmrot.next()
                        tv = mm_group(ps[:, 0:128], [(hT[:, kc, M + tb * 128:M + (tb + 1) * 128], slot[:, kc, :]) for kc in range(KC)],
                                      deps=[wtok] + plast)
                        vi, vst, vlast = vstrot.next()
                        ta = act(lambda e, vst=vst, ps=ps: e.activation(out=vst, in_=ps[:, 0:128], func=AF.Identity), [tv] + vlast)
                        td = dve(lambda e, vst=vst, tb=tb: e.tensor_copy(out=VB[:, pas * 8 + tb, :], in_=vst), [ta] + kt_free)
                        wr_toks.append(td)
                        mmrot.done(pi, [ta])
                        rd = [td]
                        tg0 = pas * TB + tb * 128
                        if tg0 >= ks:
                            rd.append(odma(vo[g][ja, tg0 - ks:tg0 - ks + 128, h, :], vst, deps=[ta]))
                        if pas == 0:
                            t_ = odma(vscr[g, h][:, tb * 128:(tb + 1) * 128], vst, deps=[ta])
                            rd.append(t_)
                            if t_ is not None:
                                scr_w[(g, h)].append(t_)
                        vstrot.done(vi, rd)
                    if pas == 0 and "nov" not in DBG and "v_nosamp" not in DBG:
                        pi, ps, plast = mmrot.next()
                        tv = mm_group(ps[0:NS, 0:128], [(hT[:, kc, M + TB:M + TB + NS], slot[:, kc, :]) for kc in range(KC)], deps=plast)
                        vi, vst, vlast = vstrot.next()
                        ta = act(lambda e, vst=vst, ps=ps: e.activation(out=vst[0:NS, :], in_=ps[0:NS, 0:128], func=AF.Identity), [tv] + vlast)
                        td = None
                        if "v_nodve" not in DBG:
                            td = dve(lambda e, vst=vst, g=g: e.tensor_copy(out=vsb[0:NS, g, :], in_=vst[0:NS, :]), [ta] + samp_last)
                        mmrot.done(pi, [ta])
                        vstrot.done(vi, [td, odma(vos[g][ja, :, h, :], vst[0:NS, :], deps=[ta])])
                    ring.release(li, tv if tv is not None else em.last["pe"])
                    prev = None
                    if pas == 1 and "noprev" not in DBG:
                        npv = {0: 1, 1: 4, 2: 8}[g]
                        kpv, kptok, kpli = ring.get(kscr[g, h][:, (8 - npv) * 128:TB], lambda s, npv=npv: s[:, 0:npv * 128], deps=scr_w[(g, h)])
                        vpv, vptok, vpli = ring.get(vscr[g, h][:, (8 - npv) * 128:TB],
                                                    lambda s, npv=npv: s[:, 0:npv * 128].rearrange("p (n e) -> p n e", n=npv), deps=scr_w[(g, h)])
                        prev = (8 - npv, kpv, vpv)
                        wr_toks += [kptok, vptok]
                    def at_S(qb, b0, bt, nkb):
                        nb = len(bt)
                        mb = MASK_BASE[g]
                        mi = mb[0] if b0 == 0 else (mb[1] if g == 1 else 8)
                        xi, pS, xlast = auxrot.next()
                        tS = None
                        for i, kb in enumerate(bt):
                            kap = KT[:, kb * 128:(kb + 1) * 128] if (pas == 0 or kb >= 8) else prev[1][:, (kb - prev[0]) * 128:(kb - prev[0] + 1) * 128]
                            tS = mm_group(pS[:, i * 128:(i + 1) * 128], [(kap, qT[:, qb * 128:(qb + 1) * 128])],
                                          deps=(wr_toks + xlast) if i == 0 else (), sig=(i == nb - 1))
                        pi_, PT, ptlast = ptrot.next()
                        ta = act(lambda e: e.activation(out=PT[:, 0:nb * 128], in_=pS[:, 0:nb * 128], func=AF.Exp, scale=SC), [tS] + ptlast)
                        auxrot.done(xi, [ta])
                        td = dve(lambda e: e.tensor_tensor(out=PT[:, 0:nb * 128], in0=PT[:, 0:nb * 128],
                                                           in1=masks[:, mi:mi + nb, :].rearrange("p a b -> p (a b)"), op=ALU.mult), [ta])
                        return (qb, b0, bt, nkb, pi_, PT, td)

                    def at_PV(stt):
                        qb, b0, bt, nkb, pi_, PT, td = stt
                        nb = len(bt)
                        t_ = None
                        for i, kb in enumerate(bt):
                            last = (g == 2 and b0 + 4 >= nkb and i == nb - 1)
                            vap = VB[:, kb, :] if (pas == 0 or kb >= 8) else prev[2][:, kb - prev[0], :]
                            mm_group(NUM(qb), [(vap, PT[:, i * 128:(i + 1) * 128])], deps=[td], sig=False, flags=(False, last))
                            t_ = mm_group(DEN(qb), [(ones[:, :], PT[:, i * 128:(i + 1) * 128])], sig=(i == nb - 1), flags=(False, last))
                        ptrot.done(pi_, [t_])
                        return t_

                    tpv = None
                    pendq = []
                    for qb in range(8):
                        qbg = pas * 8 + qb
                        kbs = [qbg - dl for dl in range(NDEL[g]) if qbg - dl >= 0]
                        for b0 in range(0, len(kbs), 4):
                            pendq.append(at_S(qb, b0, kbs[b0:b0 + 4], len(kbs)))
                            if len(pendq) > 2:
                                tpv = at_PV(pendq.pop(0))
                    while pendq:
                        tpv = at_PV(pendq.pop(0))
                    if tpv is None:
                        tpv = em.last["pe"]
                    att_reads.append(tpv)
                    kt_free = att_reads
                    if pas == 1 and "noprev" not in DBG:
                        ring.release(kpli, tpv)
                        ring.release(vpli, tpv)
                fin_tok = []
                for hf in range(2 if "noattn" not in DBG else 0):
                    td = dve(lambda e, hf=hf: e.reciprocal(out=t2b[:, :], in_=PS[:, 6 + hf, :]), [tpv])
                    td = dve(lambda e, hf=hf, h=h: e.tensor_tensor(out=OT[:, h, hf * 512:(hf + 1) * 512], in0=PS[:, 4 + hf, :], in1=t2b[:, :], op=ALU.mult), [td])
                    fin_tok = [td]
                if pas == 0 and "nosamp" not in DBG:
                    NUMS = PS[:, 4, 0:NS]
                    DENS = PS[:, 4, NS:2 * NS]
                    tlast = None
                    mm_group(PS[:, 4, 0:2 * NS], [(zer[:, :], hT[:, 0, M:M + 2 * NS])], deps=fin_tok)
                    for g in range(3):
                        L = WINS[g]
                        nblk = L // 128
                        kcs, kct, kli = ring.get(kc_in[g][ja, h], lambda s, L=L: s[:, 0:L])
                        vcs, vct, vli = ring.get(vc_in[g][ja, h], lambda s, L=L, nblk=nblk: s[:, 0:L].rearrange("p (n e) -> p n e", n=nblk))
                        xi, pS, xlast = auxrot.next()
                        tS = None
                        for kb in range(nblk):
                            tS = mm_group(pS[:, kb * NS:(kb + 1) * NS], [(kcs[:, kb * 128:(kb + 1) * 128], smallb[:, 0, g, :])],
                                          deps=[kct, vct] + xlast + fin_tok if kb == 0 else (), sig=(kb == nblk - 1))
                        pi_, PT, ptlast = ptrot.next()
                        ta = act(lambda e, PT=PT, pS=pS, nblk=nblk: e.activation(out=PT[:, 0:nblk * NS], in_=pS[:, 0:nblk * NS], func=AF.Exp, scale=SC),
                                 [tS] + ptlast)
                        auxrot.done(xi, [ta])
                        td = dve(lambda e, PT=PT, nblk=nblk, g=g: e.tensor_tensor(out=PT[:, 0:nblk * NS], in0=PT[:, 0:nblk * NS],
                                                                                 in1=smaskb[:, g, 0:nblk * NS], op=ALU.mult), [ta])
                        for kb in range(nblk):
                            first = False
                            mm_group(NUMS, [(vcs[:, kb, :], PT[:, kb * NS:(kb + 1) * NS])], deps=[td], sig=False, flags=(first, False))
                            tlast = mm_group(DENS, [(ones[:, :], PT[:, kb * NS:(kb + 1) * NS])], sig=(kb == nblk - 1), flags=(first, False))
                        ptrot.done(pi_, [tlast])
                        ring.release(kli, tlast)
                        ring.release(vli, tlast)
                        xi, pS, xlast = auxrot.next()
                        tS = mm_group(pS[0:NS, 0:NS], [(smallb[:, 1, g, :], smallb[:, 0, g, :])], deps=xlast)
                        pi_, PT, ptlast = ptrot.next()
                        ta = act(lambda e, PT=PT, pS=pS: e.activation(out=PT[0:NS, 0:NS], in_=pS[0:NS, 0:NS], func=AF.Exp, scale=SC), [tS] + ptlast)
                        auxrot.done(xi, [ta])
                        td = dve(lambda e, PT=PT, g=g: e.tensor_tensor(out=PT[0:NS, 0:NS], in0=PT[0:NS, 0:NS], in1=nmaskb[0:NS, g, :], op=ALU.mult), [ta])
                        lastg = (g == 2)
                        mm_group(NUMS, [(vsb[0:NS, g, :], PT[0:NS, 0:NS])], deps=[td], sig=False, flags=(False, lastg))
                        tlast = mm_group(DENS, [(ones[0:NS, :], PT[0:NS, 0:NS])], flags=(False, lastg))
                        ptrot.done(pi_, [tlast])
                    td = dve(lambda e: e.reciprocal(out=sm32[:, 24:28], in_=DENS), [tlast])
                    td = dve(lambda e, h=h: e.tensor_tensor(out=OT[:, h, TB:TB + NS], in0=NUMS, in1=sm32[:, 24:28], op=ALU.mult), [td])
                    fin_tok = [td]
                    samp_last = [tlast]
            em.barrier()
            accrot = Rot([bank(4), bank(5)])
            for mo in range(KC):
                slot, wtok, li = ring.get(awo[ja, mo], lambda s: s[:, 0:H * 128].rearrange("p (k c) -> p k c", k=H))
                tl = None
                for (c0, w) in tiles(pas):
                    ci, pa, clast = accrot.next()
                    tl = mm_group(pa[:, :w], [(slot[:, hh, :], OT[:, hh, c0:c0 + w]) for hh in range(H)], deps=[wtok] + clast)
                    tx = dve(lambda e, pa=pa, mo=mo, c0=c0, w=w: e.tensor_tensor(out=xT[:, mo, c0:c0 + w], in0=pa[:, :w], in1=xT[:, mo, c0:c0 + w],
                                                                                op=ALU.add), [tl])
                    accrot.done(ci, [tx])
                ring.release(li, tl)
            em.barrier()

        scr_w = {(g, h): [] for g in range(3) for h in range(H)}
        stage = [0]

        def stop():
            stage[0] += 1
            return upto is not None and stage[0] > upto

        for pas in range(2):
            xv = x_in.rearrange("(k p) t -> p k t", p=128)
            for a in range(0, KC, 4):
                em.dma(xT[:, a:a + 4, 0:TB], xv[:, a:a + 4, pas * TB:(pas + 1) * TB])
            if pas == 0:
                em.dma(xT[:, :, TB:TB + NS], xs_in.rearrange("(k p) s -> p k s", p=128))
            em.barrier()
            done = False
            for l in range(cfg.DEPTH):
                kind, j = l % 3, l // 3
                if stop():
                    done = True
                    break
                rmsnorm(pas, l)
                ffn(pas, w1i[l], w1o[l])
                if stop():
                    done = True
                    break
                rmsnorm(pas, cfg.DEPTH + l)
                if kind == 0:
                    pool_mixer(pas, j, cfg.DEPTH + l, 3 * cfg.DEPTH + j)
                elif kind == 1:
                    chunk_mixer(pas, j, 3 * cfg.DEPTH + NPL + j)
                else:
                    attn_mixer(pas, j)
                if stop():
                    done = True
                    break
                rmsnorm(pas, 2 * cfg.DEPTH + l)
                ffn(pas, w2i[l], w2o[l])
            stage[0] = 0
            yv = y_out.rearrange("(k p) t -> p k t", p=128)
            for a in range(0, KC, 4):
                em.dma(yv[:, a:a + 4, pas * TB:(pas + 1) * TB], xT[:, a:a + 4, 0:TB])
            if pas == 0:
                em.dma(ys_out.rearrange("(k p) s -> p k s", p=128), xT[:, :, TB:TB + NS])
            em.barrier()
        ring.finalize()
        final = em.all_dma_toks()

        def semof(key):
            kind, i = key
            if kind == "e":
                return esem[i]
            if kind == "d":
                return dsem[i]
            return wsem[i]

        def run(eng, name):
            for fn, waits, sig in em.ops[name]:
                for key, val in waits:
                    eng.wait_ge(semof(key), val)
                if fn is None:
                    continue
                r = fn(eng)
                if isinstance(r, tuple) and r[0] == "dma":
                    _, o, s_, k = r
                    eng.dma_start(out=o, in_=s_).then_inc(semof(k), 16)
                elif sig:
                    r.then_inc(esem[name], 1)
            if name == "sp":
                for key, val in final:
                    eng.wait_ge(semof(key), val)

        @block.tensor
        def _(t):
            run(t, "pe")

        @block.scalar
        def _(s):
            run(s, "act")

        @block.vector
        def _(v):
            run(v, "dve")

        @block.gpsimd
        def _(g):
            run(g, "pool")

        @block.sync
        def _(sp):
            run(sp, "sp")
    return nc


def coltile(W):
    K, N = W.shape
    return np.ascontiguousarray(W.reshape(K // 128, 128, N // 128, 128).transpose(2, 1, 0, 3))


def make_consts():
    c = np.zeros((128, 320), np.float32)
    cm = np.zeros((128, 1740), np.float32)
    k = np.arange(128)[:, None]
    q = np.arange(128)[None, :]
    c[:, 0:128] = (k == (q + 64) % 128)
    for gi, w in enumerate((2, 4, 8, 16)):
        c[:, 128 + gi * 16:128 + gi * 16 + 16] = 1.0 / np.minimum(w, np.arange(16) + 1)
    c[:, 192:320] = (k <= q)
    ms = []
    ms.append(q >= k)
    ms.append(q <= k)
    m4 = ((q - k) % 4 == 0)
    ms += [m4 & (q >= k), m4, m4, m4, m4 & (q <= k)]
    m16 = ((q - k) % 16 == 0)
    ms += [m16 & (q >= k), m16, m16, m16, m16]
    cm[:, 0:1536] = np.concatenate([m.astype(np.float32) for m in ms], axis=1)
    i = np.arange(128)[:, None]
    t = np.arange(4)[None, :]
    sm = [np.tile((i >= t), (1, 16)), np.tile((i % 4 == t), (1, 16)), np.tile((i % 16 == t), (1, 16))]
    cm[:, 1536:1536 + 192] = np.concatenate([m.astype(np.float32) for m in sm], axis=1)
    kk = np.arange(4)[:, None]
    nm = [(kk <= t), (kk == t), (kk == t)]
    cm[0:4, 1728:1740] = np.concatenate([m.astype(np.float32) for m in nm], axis=1)
    return c, cm


def make_rope():
    half = 64
    freqs = (np.float32(10000.0) ** (-2.0 * np.arange(half, dtype=np.float32) / np.float32(128))).astype(np.float32)
    pos = np.concatenate([np.arange(SEQ), PAST + np.arange(NS)]).astype(np.float32)
    ang = (pos[None, :] * freqs[:, None]).astype(np.float32)
    cos = np.cos(ang).astype(np.float32)
    sin = np.sin(ang).astype(np.float32)
    r = np.zeros((128, 2, SEQ + NS), np.float32)
    r[:64, 0] = cos
    r[64:, 0] = cos
    r[:64, 1] = -sin
    r[64:, 1] = sin
    return r


def prep_inputs(cfg, inp):
    D, KC, NJ, H = cfg.D, cfg.KC, cfg.NJ, cfg.H
    f = lambda a: np.ascontiguousarray(np.asarray(a, dtype=np.float32))
    sh = {}
    vl = [inp["norm_ffn1"], inp["norm_mix"], inp["norm_ffn2"], inp["pool_scale"], inp["chunk_v_norm"]]
    allv = np.concatenate([f(v) for v in vl], axis=0)
    sh["vecs"] = np.ascontiguousarray(allv.reshape(cfg.NV, KC, 128).transpose(2, 0, 1))
    sh["consts"], sh["cmask"] = make_consts()
    sh["rope"] = make_rope()

    def win(w):
        w = f(w)
        out = np.empty((cfg.DEPTH, NJ, 128, 2, KC, 128), np.float32)
        for l in range(cfg.DEPTH):
            ct = coltile(w[l])
            out[l, :, :, 0] = ct[:NJ]
            out[l, :, :, 1] = ct[NJ:]
        return out.reshape(cfg.DEPTH, NJ, 128, 2 * KC * 128)

    sh["w1i"] = win(inp["ffn1_w_in"])
    sh["w2i"] = win(inp["ffn2_w_in"])
    sh["w1o"] = f(inp["ffn1_w_out"]).reshape(cfg.DEPTH, NJ, 128, D)
    sh["w2o"] = f(inp["ffn2_w_out"]).reshape(cfg.DEPTH, NJ, 128, D)
    sh["pool_w"] = f(inp["pool_w"])
    cw = f(inp["chunk_w_in"])
    sh["cw_v"] = np.ascontiguousarray(cw[:, :, D:])
    sh["cw_u"] = np.stack([coltile(cw[j, :, :D]).reshape(KC, 128, KC * 128) for j in range(cfg.NCL)])
    sh["cw_o"] = np.stack([coltile(f(inp["chunk_w_out"])[j]).reshape(KC, 128, KC * 128) for j in range(cfg.NCL)])
    sh["c_vg"] = f(inp["chunk_v_norm"])
    sh["c_ws"] = np.ascontiguousarray(f(inp["chunk_w_s"]).transpose(0, 3, 1, 2)).reshape(cfg.NCL, 128, 8 * 128)
    sh["c_bs"] = f(inp["chunk_b_s"]).reshape(cfg.NCL, 8 * 128)
    sh["a_qkv"] = np.stack([coltile(f(inp["attn_w_qkv"])[j]).reshape(9 * H, 128, KC * 128) for j in range(cfg.NAL)])
    sh["a_wo"] = np.stack([coltile(f(inp["attn_w_out"])[j]).reshape(KC, 128, H * 128) for j in range(cfg.NAL)])
    sh["a_gn"] = np.ascontiguousarray(np.stack([f(inp["attn_q_norm"]), f(inp["attn_k_norm"])], axis=-1))
    xp = f(inp["x_prompt"])
    xs = f(inp["x_sample"])
    st = f(inp["state_pool"])
    caches = [f(inp["cache_kv_g0"]), f(inp["cache_kv_g1"]), f(inp["cache_kv_g2"])]
    maps = []
    zx = np.zeros((D, SEQ), np.float32)
    for c in range(cfg.NCORES):
        m = dict(sh)
        m["x_in"] = np.ascontiguousarray(xp[PCORE.index(c)].T) if c in PCORE else zx
        m["xs_in"] = np.ascontiguousarray(xs[c].T)
        m["pool_state"] = np.ascontiguousarray(st[:, c].transpose(0, 2, 1))
        for g in range(3):
            ck = caches[g][:, c]
            m["kcache%d" % g] = np.ascontiguousarray(ck[:, :, 0].transpose(0, 2, 3, 1))
            L = ck.shape[1]
            v = ck[:, :, 1].reshape(cfg.NAL, L // 128, 128, H, 128).transpose(0, 3, 2, 1, 4)
            m["vcache%d" % g] = np.ascontiguousarray(v).reshape(cfg.NAL, H, 128, L)
        maps.append(m)
    return maps


def assemble(cfg, res, nb=4):
    H = cfg.H
    R = lambda c, k: np.asarray(res[c][k], dtype=np.float32)
    DB = cfg.NCORES
    PC = PCORE
    y_p = np.stack([R(PC[b], "y").T for b in range(nb)])
    y_s = np.stack([R(c, "ys").T for c in range(DB)])
    pool_p = np.stack([R(PC[b], "pool_p").transpose(0, 2, 1) for b in range(nb)], axis=1)
    pool_s = np.stack([R(c, "pool_s").transpose(0, 2, 1) for c in range(DB)], axis=1)
    chunk_v = np.stack([R(c, "chunk_v") for c in range(DB)], axis=1)
    outs = [y_p, y_s, pool_p, pool_s, chunk_v]
    for g in range(3):
        kp = np.stack([R(PC[b], "ko%d" % g).transpose(0, 3, 1, 2) for b in range(nb)], axis=1)
        vp = np.stack([R(PC[b], "vo%d" % g) for b in range(nb)], axis=1)
        outs.append(np.ascontiguousarray(np.stack([kp, vp], axis=3)))
        ksm = np.stack([R(c, "kos%d" % g).transpose(0, 3, 1, 2) for c in range(DB)], axis=1)
        vsm = np.stack([R(c, "vos%d" % g) for c in range(DB)], axis=1)
        outs.append(np.ascontiguousarray(np.stack([ksm, vsm], axis=3)))
    return tuple(outs)


def kernel(**inputs):
    cfg = Cfg()
    maps = prep_inputs(cfg, inputs)
    nc = build(cfg)
    res = run_bass_kernel_spmd(nc, maps, core_ids=list(range(cfg.NCORES)))
    return assemble(cfg, res.results)
```
